# Optimizing a Trainium2 kernel written in Bass

```python
import math
import jax, jax.numpy as jnp
from jax import lax
import numpy as np

D_MODEL = 2048
BATCH = 1
SEQ = 8192
DEPTH = 2
DEC_BATCH = 128
DEC_SEQ = 1
PAST_LEN = 8192
PAGE_SIZE = 128

HEAD_DIM = 128
RET_HEADS = D_MODEL // 2 // HEAD_DIM
RET_DK = HEAD_DIM
RET_DV = HEAD_DIM
RET_W = RET_HEADS * RET_DV
RET_CHUNK = 128
SWA_HEADS = D_MODEL // 2 // HEAD_DIM
SWA_KV_HEADS = 2
SWA_GROUP = SWA_HEADS // SWA_KV_HEADS
SWA_W = SWA_HEADS * HEAD_DIM
SWA_KV_W = SWA_KV_HEADS * HEAD_DIM
WINDOW = 128
MIX_W = RET_W + SWA_W
IN_SIZES = (RET_W, RET_W, RET_W, RET_W, SWA_W, SWA_KV_W, SWA_KV_W)
IN_W = RET_W * 4 + SWA_W + 2 * SWA_KV_W
MEM_LEN = 256
MEM_HEADS = 4
MEM_HD = 128
MEM_W = MEM_HEADS * MEM_HD
D_FF = ((8 * D_MODEL + 3 * 256 - 1) // (3 * 256)) * 256
ROPE_BASE = 10000.0
EPS = 1e-6

kernel_name = "hymba_retention_swa_sink_memxattn_step"


def rmsnorm(x, g):
    xf = x.astype(jnp.float32)
    y = xf * lax.rsqrt(jnp.mean(xf * xf, axis=-1, keepdims=True) + EPS)
    return (y * g.astype(jnp.float32)).astype(x.dtype)


def rotary(x, pos):
    half = x.shape[-1] // 2
    inv = ROPE_BASE ** (-jnp.arange(half, dtype=jnp.float32) / half)
    ang = pos.astype(jnp.float32)[:, None] * inv[None, :]
    cos = jnp.cos(ang)[None, :, None, :]
    sin = jnp.sin(ang)[None, :, None, :]
    xf = x.astype(jnp.float32)
    x1, x2 = xf[..., :half], xf[..., half:]
    return jnp.concatenate([x1 * cos - x2 * sin, x1 * sin + x2 * cos], axis=-1).astype(x.dtype)


def retention_log_decay():
    return jnp.log1p(-jnp.exp2(-5.0 - jnp.arange(RET_HEADS, dtype=jnp.float32)))


def retention(q, k, v, r0):
    B, L, H, DK = q.shape
    DV = v.shape[-1]
    f32 = jnp.float32
    C = RET_CHUNK if L % RET_CHUNK == 0 else L
    nc = L // C
    lg = retention_log_decay()
    idx = jnp.arange(C, dtype=f32)
    diff = idx[:, None] - idx[None, :]
    dmask = jnp.where(diff[None] >= 0, jnp.exp(lg[:, None, None] * jnp.maximum(diff, 0.0)[None]), 0.0)
    xi = jnp.exp(lg[:, None] * (idx + 1.0)[None])
    zeta = jnp.exp(lg[:, None] * (C - 1.0 - idx)[None])
    g_chunk = jnp.exp(lg * C)
    qc = q.astype(f32).reshape(B, nc, C, H, DK)
    kc = k.astype(f32).reshape(B, nc, C, H, DK)
    vc = v.astype(f32).reshape(B, nc, C, H, DV)
    s = jnp.einsum('bnchd,bnmhd->bnhcm', qc, kc) * dmask
    inner = jnp.einsum('bnhcm,bnmhv->bnchv', s, vc)
    kv = jnp.einsum('bnmhd,hm,bnmhv->nbhdv', kc, zeta, vc)

    def step(r, kv_n):
        return g_chunk[None, :, None, None] * r + kv_n, r

    r_last, r_prev = lax.scan(step, r0.astype(f32), kv)
    cross = jnp.einsum('bnchd,hc,nbhdv->bnchv', qc, xi, r_prev)
    out = (inner + cross).reshape(B, L, H, DV)
    return out, r_last


def sink_softmax(s, mask, sink):
    s = jnp.where(mask, s, -jnp.inf)
    sk = jnp.broadcast_to(sink.astype(jnp.float32).reshape(SWA_KV_HEADS, SWA_GROUP, 1, 1), s.shape[:-1] + (1,))
    p = jax.nn.softmax(jnp.concatenate([s, sk], axis=-1), axis=-1)
    return p[..., :-1]


def swa_prompt(q, k, v, sink):
    B, L, _, hd = q.shape
    W = WINDOW
    nb = L // W
    qb = q.reshape(B, nb, W, SWA_KV_HEADS, SWA_GROUP, hd)
    kb = k.reshape(B, nb, W, SWA_KV_HEADS, hd)
    vb = v.reshape(B, nb, W, SWA_KV_HEADS, hd)
    kk = jnp.concatenate([jnp.concatenate([jnp.zeros_like(kb[:, :1]), kb[:, :-1]], axis=1), kb], axis=2)
    vv = jnp.concatenate([jnp.concatenate([jnp.zeros_like(vb[:, :1]), vb[:, :-1]], axis=1), vb], axis=2)
    blk = jnp.arange(nb)[:, None] * W
    qpos = blk + jnp.arange(W)[None, :]
    kpos = blk - W + jnp.arange(2 * W)[None, :]
    d = qpos[:, :, None] - kpos[:, None, :]
    mask = (d >= 0) & (d <= WINDOW) & (kpos[:, None, :] >= 0)
    s = jnp.einsum('bnqkgd,bnskd->bnkgqs', qb, kk).astype(jnp.float32) * (hd ** -0.5)
    p = sink_softmax(s, mask[None, :, None, None], sink)
    o = jnp.einsum('bnkgqs,bnskd->bnqkgd', p.astype(vv.dtype), vv)
    return o.reshape(B, L, SWA_W)


def swa_decode(q, k, v, ck, cv, sink):
    B, T, _, hd = q.shape
    kk = jnp.concatenate([ck.astype(k.dtype), k], axis=1)
    vv = jnp.concatenate([cv.astype(v.dtype), v], axis=1)
    qpos = PAST_LEN + jnp.arange(T)
    kpos = jnp.concatenate([PAST_LEN - WINDOW + jnp.arange(WINDOW), PAST_LEN + jnp.arange(T)])
    d = qpos[:, None] - kpos[None, :]
    mask = (d >= 0) & (d <= WINDOW)
    qg = q.reshape(B, T, SWA_KV_HEADS, SWA_GROUP, hd)
    s = jnp.einsum('btkgd,bskd->bkgts', qg, kk).astype(jnp.float32) * (hd ** -0.5)
    p = sink_softmax(s, mask, sink)
    o = jnp.einsum('bkgts,bskd->btkgd', p.astype(vv.dtype), vv).reshape(B, T, SWA_W)
    return o, kk[:, -WINDOW:], vv[:, -WINDOW:]


def mixer(h, pos, w_in, ret_gn, sink, w_out, r0, ck, cv):
    B, L, _ = h.shape
    proj = h @ w_in
    splits = [int(s) for s in np.cumsum(IN_SIZES)[:-1]]
    rq, rk, rv, rg, sq, sk, sv = jnp.split(proj, splits, axis=-1)
    rq = rotary(rq.reshape(B, L, RET_HEADS, RET_DK), pos)
    rk = rotary(rk.reshape(B, L, RET_HEADS, RET_DK), pos) * (RET_DK ** -0.5)
    rv = rv.reshape(B, L, RET_HEADS, RET_DV)
    if r0 is None:
        r0 = jnp.zeros((B, RET_HEADS, RET_DK, RET_DV), jnp.float32)
    ro, r_new = retention(rq, rk, rv, r0)
    ro = ro * lax.rsqrt(jnp.mean(ro * ro, axis=-1, keepdims=True) + EPS)
    ro = ro.reshape(B, L, RET_W) * ret_gn.astype(jnp.float32)
    ro = (jax.nn.silu(rg.astype(jnp.float32)) * ro).astype(h.dtype)
    sq = sq.reshape(B, L, SWA_HEADS, HEAD_DIM)
    sk = sk.reshape(B, L, SWA_KV_HEADS, HEAD_DIM)
    sv = sv.reshape(B, L, SWA_KV_HEADS, HEAD_DIM)
    if ck is None:
        so = swa_prompt(sq, sk, sv, sink)
        nk, nv = sk[:, -WINDOW:], sv[:, -WINDOW:]
    else:
        so, nk, nv = swa_decode(sq, sk, sv, ck, cv, sink)
    y = jnp.concatenate([ro, so.astype(h.dtype)], axis=-1) @ w_out
    return y, r_new.astype(h.dtype), nk, nv


def cross_attn(h, mk, mv, wq, wo):
    B, L, _ = h.shape
    q = (h @ wq).reshape(B, L, MEM_HEADS, MEM_HD)
    s = jnp.einsum('blhd,bmhd->bhlm', q, mk.astype(q.dtype)).astype(jnp.float32) * (MEM_HD ** -0.5)
    p = jax.nn.softmax(s, axis=-1)
    o = jnp.einsum('bhlm,bmhd->blhd', p.astype(h.dtype), mv.astype(h.dtype)).reshape(B, L, MEM_W)
    return o @ wo


def run_trunk(x, pos, P, mem=None, state_ret=None, cache_win_k=None, cache_win_v=None,
              cache_mem_k=None, cache_mem_v=None):
    rs, ks, vs, mks, mvs = [], [], [], [], []
    for l in range(DEPTH):
        h = rmsnorm(x, P['g_mix'][l])
        r0 = None if state_ret is None else state_ret[l]
        ck = None if cache_win_k is None else cache_win_k[l]
        cv = None if cache_win_v is None else cache_win_v[l]
        y, r_new, nk, nv = mixer(h, pos, P['w_in'][l], P['ret_gn'][l], P['sinks'][l], P['w_out'][l], r0, ck, cv)
        x = x + y
        if mem is not None:
            mn = rmsnorm(mem, P['g_mem'][l])
            B = mem.shape[0]
            mk = (mn @ P['wk_c'][l]).reshape(B, MEM_LEN, MEM_HEADS, MEM_HD)
            mv = (mn @ P['wv_c'][l]).reshape(B, MEM_LEN, MEM_HEADS, MEM_HD)
        else:
            mk, mv = cache_mem_k[l], cache_mem_v[l]
        x = x + cross_attn(rmsnorm(x, P['g_cross'][l]), mk, mv, P['wq_c'][l], P['wo_c'][l])
        h = rmsnorm(x, P['g_ffn'][l])
        x = x + (jax.nn.silu(h @ P['w_gate'][l]) * (h @ P['w_up'][l])) @ P['w_down'][l]
        rs.append(r_new); ks.append(nk); vs.append(nv); mks.append(mk); mvs.append(mv)
    x = rmsnorm(x, P['g_final'])
    return x, rs, ks, vs, mks, mvs


def setup_inputs(seed: int = 0) -> dict:
    key = jax.random.key(seed)
    ks = jax.random.split(key, 32)
    f32 = jnp.float32
    nrm = lambda k, shape, scale: jax.random.normal(k, shape, f32) * scale
    gain = lambda k, shape: 1.0 + 0.02 * jax.random.normal(k, shape, f32)
    return {
        "x_prompt": nrm(ks[0], (BATCH, SEQ, D_MODEL), 1.0),
        "x_sample": nrm(ks[1], (DEC_BATCH, DEC_SEQ, D_MODEL), 1.0),
        "mem_prompt": nrm(ks[2], (BATCH, MEM_LEN, D_MODEL), 1.0),
        "state_ret": nrm(ks[3], (DEPTH, DEC_BATCH, RET_HEADS, RET_DK, RET_DV), 0.5),
        "cache_win_k": nrm(ks[4], (DEPTH, DEC_BATCH, WINDOW, SWA_KV_HEADS, HEAD_DIM), 1.0),
        "cache_win_v": nrm(ks[5], (DEPTH, DEC_BATCH, WINDOW, SWA_KV_HEADS, HEAD_DIM), 1.0),
        "cache_mem_k": nrm(ks[6], (DEPTH, DEC_BATCH, MEM_LEN, MEM_HEADS, MEM_HD), 1.0),
        "cache_mem_v": nrm(ks[7], (DEPTH, DEC_BATCH, MEM_LEN, MEM_HEADS, MEM_HD), 1.0),
        "g_mix": gain(ks[8], (DEPTH, D_MODEL)),
        "w_in": nrm(ks[9], (DEPTH, D_MODEL, IN_W), D_MODEL ** -0.5),
        "ret_gn": gain(ks[10], (DEPTH, RET_W)),
        "sinks": nrm(ks[11], (DEPTH, SWA_HEADS), 0.5),
        "w_out": nrm(ks[12], (DEPTH, MIX_W, D_MODEL), MIX_W ** -0.5),
        "g_cross": gain(ks[13], (DEPTH, D_MODEL)),
        "g_mem": gain(ks[14], (DEPTH, D_MODEL)),
        "wq_c": nrm(ks[15], (DEPTH, D_MODEL, MEM_W), D_MODEL ** -0.5),
        "wk_c": nrm(ks[16], (DEPTH, D_MODEL, MEM_W), D_MODEL ** -0.5),
        "wv_c": nrm(ks[17], (DEPTH, D_MODEL, MEM_W), D_MODEL ** -0.5),
        "wo_c": nrm(ks[18], (DEPTH, MEM_W, D_MODEL), MEM_W ** -0.5),
        "g_ffn": gain(ks[19], (DEPTH, D_MODEL)),
        "w_gate": nrm(ks[20], (DEPTH, D_MODEL, D_FF), D_MODEL ** -0.5),
        "w_up": nrm(ks[21], (DEPTH, D_MODEL, D_FF), D_MODEL ** -0.5),
        "w_down": nrm(ks[22], (DEPTH, D_FF, D_MODEL), D_FF ** -0.5),
        "g_final": gain(ks[23], (D_MODEL,)),
    }


def reference(x_prompt, x_sample, mem_prompt, state_ret, cache_win_k, cache_win_v, cache_mem_k, cache_mem_v,
              g_mix, w_in, ret_gn, sinks, w_out, g_cross, g_mem, wq_c, wk_c, wv_c, wo_c,
              g_ffn, w_gate, w_up, w_down, g_final):
    P = dict(g_mix=g_mix, w_in=w_in, ret_gn=ret_gn, sinks=sinks, w_out=w_out, g_cross=g_cross,
             g_mem=g_mem, wq_c=wq_c, wk_c=wk_c, wv_c=wv_c, wo_c=wo_c, g_ffn=g_ffn,
             w_gate=w_gate, w_up=w_up, w_down=w_down, g_final=g_final)
    pos_p = jnp.arange(x_prompt.shape[1], dtype=jnp.int32)
    y_prompt, rp, kp, vp, mkp, mvp = run_trunk(x_prompt, pos_p, P, mem=mem_prompt)
    pos_s = PAST_LEN + jnp.arange(x_sample.shape[1], dtype=jnp.int32)
    y_sample, rs, ks_, vs_, _, _ = run_trunk(x_sample, pos_s, P, state_ret=state_ret,
                                             cache_win_k=cache_win_k, cache_win_v=cache_win_v,
                                             cache_mem_k=cache_mem_k, cache_mem_v=cache_mem_v)
    return (y_prompt, y_sample,
            jnp.stack(rp), jnp.stack(kp), jnp.stack(vp), jnp.stack(mkp), jnp.stack(mvp),
            jnp.stack(rs), jnp.stack(ks_), jnp.stack(vs_))
```

```python
from contextlib import ExitStack
import os
import numpy as np
import ml_dtypes
import concourse.bass as bass
import concourse.mybir as mybir
from concourse.bass_utils import run_bass_kernel_spmd

F32 = mybir.dt.float32
BF16 = mybir.dt.bfloat16
ALU = mybir.AluOpType
AF = mybir.ActivationFunctionType

NCORES = 8
D = 2048
KC = 16
NT = 1024
NB = 16
T = NT + NB
INW = 5632
DFF = 5632
EPS = 1e-6
TT = [(0, 352), (352, 352), (704, 336)]
SC = 128 ** -0.5


class Sched:
    ENG = ("pe", "act", "dve", "pool", "sp")

    def __init__(self, nc, signaled=None):
        self.nc = nc
        self.live = signaled is not None
        self.signaled_in = signaled
        self.eng = {"pe": nc.tensor, "act": nc.scalar, "dve": nc.vector, "pool": nc.gpsimd, "sp": nc.sync}
        self.n = 0
        self.last_w = {}
        self.readers = {}
        self.signaled = {}
        self.is_pe = {}
        self.is_dma = {}
        self.op_eng = {}
        self.pending_bar = {e: set() for e in self.ENG}
        self.since_bar = {e: None for e in self.ENG}
        self.dma_since_bar = []
        if self.live:
            self.esem = {e: nc.alloc_semaphore(name="sem_" + e) for e in ("pe", "act", "dve", "pool")}
            self.dsem = {}
            self.ecount = {e: 0 for e in self.esem}
            self.dcount = {}
            self.known = {e: {} for e in self.ENG}
            self.ticket = {}
            self.done = {}
            self.nwait = 0

    def barrier(self):
        s = set(self.dma_since_bar)
        for e in self.ENG:
            if self.since_bar[e] is not None:
                s.add(self.since_bar[e])
        for e in self.ENG:
            self.pending_bar[e] |= s
        self.dma_since_bar = []

    def add(self, eng, fn, r=(), w=(), dma=None, inc=16, nobar=False):
        i = self.n
        self.n += 1
        w = list(w) + [k for k in r if isinstance(k, tuple) and k[0] == "ps" and k not in w]
        d = set()
        for k in r:
            j = self.last_w.get(k)
            if j is not None:
                d.add(j)
        for k in w:
            j = self.last_w.get(k)
            if j is not None:
                d.add(j)
            for j in self.readers.get(k, ()):
                d.add(j)
        d |= self.pending_bar[eng]
        self.pending_bar[eng] = set()
        d.discard(i)
        pe_plain = eng == "pe" and dma is None
        self.is_pe[i] = pe_plain
        self.is_dma[i] = dma is not None
        self.op_eng[i] = eng
        if pe_plain:
            d = {j for j in d if not self.is_pe[j]}
        latest = {}
        keep = set()
        for j in d:
            if self.is_dma[j]:
                keep.add(j)
            else:
                ej = self.op_eng[j]
                if ej not in latest or latest[ej] < j:
                    latest[ej] = j
        d = keep | set(latest.values())
        for j in d:
            self.signaled[j] = True
        for k in w:
            self.last_w[k] = i
            self.readers[k] = []
        for k in r:
            self.readers.setdefault(k, []).append(i)
        if dma is not None:
            if not nobar:
                self.dma_since_bar.append(i)
        else:
            self.since_bar[eng] = i
        if not self.live:
            return
        nc = self.nc
        E = self.eng[eng]
        kn = self.known[eng]
        best = {}
        for j in d:
            sem, val = self.ticket[j]
            if sem not in best or best[sem][0] < val:
                best[sem] = (val, j)
        for sem, (val, j) in sorted(best.items(), key=lambda kv: kv[1][1]):
            if kn.get(sem, 0) < val:
                E.wait_ge(sem, val)
                self.nwait += 1
        for j in d:
            for s2, v2 in self.done[j].items():
                if kn.get(s2, 0) < v2:
                    kn[s2] = v2
        ins = fn(E)
        if dma is not None:
            if dma not in self.dsem:
                self.dsem[dma] = nc.alloc_semaphore(name="dsem_%d" % len(self.dsem))
                self.dcount[dma] = 0
            self.dcount[dma] += inc
            ins.then_inc(self.dsem[dma], inc)
            self.ticket[i] = (self.dsem[dma], self.dcount[dma])
            dn = dict(kn)
            dn[self.dsem[dma]] = self.dcount[dma]
            self.done[i] = dn
        elif self.signaled_in.get(i, False):
            self.ecount[eng] += 1
            ins.then_inc(self.esem[eng], 1)
            self.ticket[i] = (self.esem[eng], self.ecount[eng])
            dn = dict(kn)
            dn[self.esem[eng]] = self.ecount[eng]
            self.done[i] = dn

    def finish(self):
        if not self.live:
            return
        for key, s in self.dsem.items():
            self.nc.sync.wait_ge(s, self.dcount[key])
        for e in ("pe", "act", "dve", "pool"):
            if self.ecount[e]:
                self.nc.sync.wait_ge(self.esem[e], self.ecount[e])


CF = {}
CB = {}


def _layout():
    off = 0
    for name, n in (("zeta", 8), ("zetaE", 64), ("gpow", 8), ("gamma", 8), ("coefS", 64), ("coefF", 64),
                    ("sel", 8), ("ident", 128), ("oh16", 16)):
        CF[name] = (off, n)
        off += n
    CF["_n"] = off
    off = 0
    for name, n in (("ident", 128), ("ones", 128), ("pswap", 128), ("cos", T), ("sin", T), ("dmask", 1024),
                    ("xi", 1024), ("mcur", 128), ("mprev", 128), ("mprev0", 128), ("oh16", 16)):
        CB[name] = (off, n)
        off += n
    CB["_n"] = off


_layout()


def make_consts(core):
    f = np.zeros((128, CF["_n"]), np.float64)
    b = np.zeros((128, CB["_n"]), np.float64)
    lg = np.log1p(-np.exp2(-5.0 - np.arange(8, dtype=np.float64)))
    gam = np.exp(lg)
    g = np.exp(lg * 128)
    m = np.arange(128)

    def F(name):
        o, n = CF[name]
        return f[:, o:o + n]

    def B(name):
        o, n = CB[name]
        return b[:, o:o + n]

    F("zeta")[:] = np.exp(lg[None, :] * (127 - m)[:, None]) * SC
    F("zetaE")[:] = (F("zeta")[:, :, None] * np.exp(lg[None, :, None] * 128 * (7 - np.arange(8))[None, None, :])).reshape(128, 64)
    F("gpow")[:] = g[None, :]
    F("gamma")[:] = gam[None, :]
    cs = np.zeros((8, 8))
    cf = np.zeros((8, 8))
    for cp in range(8):
        if cp < core:
            cs[cp] = np.exp(lg * 128 * 8 * (core - 1 - cp))
        cf[cp] = np.exp(lg * 128 * 8 * (7 - cp))
    F("coefS")[:] = cs.reshape(1, 64)
    F("coefF")[:] = cf.reshape(1, 64)
    sel = np.zeros(8)
    if core > 0:
        sel[core - 1] = 1.0
    F("sel")[:] = sel[None, :]
    F("ident")[:] = np.eye(128)
    F("oh16")[:] = np.eye(128)[:, :16] * SC
    B("ident")[:] = np.eye(128)
    B("ones")[:] = 1.0
    ps = np.zeros((128, 128))
    for mm in range(128):
        if mm < 64:
            ps[mm + 64, mm] = -1.0
        else:
            ps[mm - 64, mm] = 1.0
    B("pswap")[:] = ps
    half = 64
    inv = (np.float32(10000.0) ** (-(np.arange(half, dtype=np.float32)) / np.float32(half))).astype(np.float32)
    pos = np.concatenate([core * NT + np.arange(NT), np.full(NB, 8192)]).astype(np.float32)
    ang = (pos[:, None] * inv[None, :]).astype(np.float32)
    B("cos")[:] = np.concatenate([np.cos(ang), np.cos(ang)], 1).T
    B("sin")[:] = np.concatenate([np.sin(ang), np.sin(ang)], 1).T
    c = np.arange(128)
    diff = c[None, :] - m[:, None]
    dm = np.where(diff[:, None, :] >= 0, np.exp(lg[None, :, None] * np.maximum(diff, 0)[:, None, :]), 0.0) * SC
    B("dmask")[:] = dm.reshape(128, 1024)
    xi = np.exp(lg[:, None] * (c + 1.0)[None, :])
    B("xi")[:] = np.broadcast_to(xi.reshape(1, 1024), (128, 1024))
    B("mcur")[:] = (m[:, None] <= c[None, :])
    B("mprev")[:] = (m[:, None] >= c[None, :])
    B("mprev0")[:] = (m[:, None] >= c[None, :]) if core > 0 else 0.0
    B("oh16")[:] = np.eye(128)[:, :16]
    return f.astype(np.float32), b.astype(np.float32).astype(ml_dtypes.bfloat16)


IN_SPEC = [("xp", (NT, D)), ("xs", (NB, D)), ("sret", (2, NB, 8, 128, 128)),
           ("cwk", (2, NB, 128, 256)), ("cwv", (2, NB, 128, 256)), ("cmk", (2, NB, 256, 512)), ("cmv", (2, NB, 256, 512)),
           ("cF", (128, CF["_n"])), ("cB", (128, CB["_n"])),
           ("mem", (256, D)), ("gvec", (128, 176)),
           ("w_in", (2, D, INW)), ("w_out", (2, D, D)), ("wq_c", (2, D, 512)), ("wk_c", (2, D, 512)), ("wv_c", (2, D, 512)),
           ("wo_c", (2, 512, D)), ("w_gate", (2, D, DFF)), ("w_up", (2, D, DFF)), ("w_down", (2, DFF, D))]
N_PERCORE = 9
OUT_SPEC = [("yp", (NT, D)), ("ys", (NB, D)), ("retp", (2, 8, 128, 128)), ("wkp", (2, 128, 256)), ("wvp", (2, 128, 256)),
            ("mkp", (2, 256, 512)), ("mvp", (2, 256, 512)), ("rets", (2, NB, 8, 128, 128)), ("wks", (2, NB, 128, 256)),
            ("wvs", (2, NB, 128, 256))]


def _offsets(spec):
    off = {}
    o = 0
    for name, shape in spec:
        n = int(np.prod(shape))
        off[name] = (o, n, shape)
        o += n
    return off, o


def _view(flat_ap, off, name):
    o, n, shape = off[name]
    v = flat_ap[o:o + n]
    if len(shape) == 1:
        return v
    letters = "abcdefg"[:len(shape)]
    pat = "(" + " ".join(letters) + ") -> " + " ".join(letters)
    kw = {letters[i]: shape[i] for i in range(1, len(shape))}
    return v.rearrange(pat, **kw)

def tiles_of(c0, c1):
    return [ti for ti, (t0, tn) in enumerate(TT) if t0 < c1 and c0 < t0 + tn]


class Ctx:
    pass


def build(nc, S, stage=99, dbg=()):
    C = Ctx()
    C.uid = 0
    import os
    C.use_ag = not os.environ.get("NO_AG")
    es = ExitStack()

    def din(name, shape, dt=F32):
        return nc.dram_tensor(name, list(shape), dt, kind="ExternalInput")

    def dout(name, shape, dt=F32):
        return nc.dram_tensor(name, list(shape), dt, kind="ExternalOutput")

    def sb(stack, name, shape, dt):
        C.uid += 1
        return stack.enter_context(nc.sbuf_tensor("s%d_%s" % (C.uid, name), list(shape), dt))

    in_off, in_tot = _offsets(IN_SPEC)
    out_off, out_tot = _offsets(OUT_SPEC)
    IN = nc.dram_tensor("IN", [in_tot], F32, kind="ExternalInput").ap()
    OUT = nc.dram_tensor("OUT", [out_tot], F32, kind="ExternalOutput").ap()
    V = lambda name: _view(IN, in_off, name)
    O = lambda name: _view(OUT, out_off, name)
    xp = V("xp"); xs = V("xs"); mem = V("mem"); sret = V("sret"); cwk = V("cwk"); cwv = V("cwv"); cmk = V("cmk"); cmv = V("cmv")
    w_in = V("w_in"); w_out = V("w_out"); wq_c = V("wq_c"); wk_c = V("wk_c"); wv_c = V("wv_c"); wo_c = V("wo_c")
    w_gate = V("w_gate"); w_up = V("w_up"); w_down = V("w_down")
    gvec_d = V("gvec"); cF_d = V("cF"); cB_d = V("cB")
    yp = O("yp"); ys = O("ys"); retp = O("retp"); wkp = O("wkp"); wvp = O("wvp"); mkp = O("mkp"); mvp = O("mvp")
    rets = O("rets"); wks = O("wks"); wvs = O("wvs")
    ag1_in_l = [nc.dram_tensor("ag1_in%d" % i, [128, 512], F32) for i in range(2)]
    ag1_out_l = [nc.dram_tensor("ag1_out%d" % i, [8 * 128, 512], F32) for i in range(2)]
    ag2_in_l = [nc.dram_tensor("ag2_in%d" % i, [128, 1024], F32) for i in range(2)]
    ag2_out_l = [nc.dram_tensor("ag2_out%d" % i, [8 * 128, 1024], F32) for i in range(2)]

    xT = sb(es, "xT", [128, KC, T], F32)
    hT = sb(es, "hT", [128, KC, T], BF16)
    NSLOT = 2
    wsl = [sb(es, "wsl%d" % i, [128, 16 * 256], BF16) for i in range(NSLOT)]
    cF = sb(es, "cF", [128, CF["_n"]], F32)
    cB = sb(es, "cB", [128, CB["_n"]], BF16)
    gvec = sb(es, "gvec", [128, 176], F32)
    sqb = sb(es, "sqb", [128, 2, 4, 352], BF16)
    lnb = sb(es, "lnb", [128, 2, 352], F32)
    pb = [nc.alloc_psum_tensor("ps%d" % i, [128, 512], F32) for i in range(8)]
    C.bank = 0
    C.wslot = 0
    C.cnt = 0

    C.ring = list(range(8))

    def bank():
        b = C.ring.pop(0)
        C.ring.append(b)
        return b

    def hold():
        return C.ring.pop(0)

    def release(b):
        C.ring.append(b)

    def cf(name, a=0, n=None):
        o, nn = CF[name]
        n = nn - a if n is None else n
        return cF[:, o + a:o + a + n]

    def cb(name, a=0, n=None):
        o, nn = CB[name]
        n = nn - a if n is None else n
        return cB[:, o + a:o + a + n]

    def G(idx, kc):
        return gvec[:, idx * 16 + kc:idx * 16 + kc + 1]

    PS = lambda b: ("ps", b)
    add = S.add

    add("sp", lambda e: e.dma_start(out=cF[:], in_=cF_d), w=["cF"], dma="c0")
    add("pool", lambda e: e.dma_start(out=cB[:], in_=cB_d), w=["cB"], dma="c1")
    add("sp", lambda e: e.dma_start(out=gvec[:], in_=gvec_d), w=["gvec"], dma="c2")
    CK = ["cF", "cB", "gvec"]

    dumps = {}

    def dump(name, t, keys):
        if name not in dbg:
            return
        d = nc.dram_tensor("dbg_" + name, list(t.shape), t.dtype, kind="ExternalOutput")
        add("sp", lambda e: e.dma_start(out=d.ap(), in_=t[:] if not isinstance(t, bass.AP) else t), r=keys, dma="dbg")
        dumps[name] = d

    def load_tokens(stack, src_list, dst, dstname, ncols_list):
        xtok = sb(stack, "xtok", [128, 2, D], F32)
        col = 0
        for i, (src, nrow) in enumerate(src_list):
            sl = i % 2
            add("sp", lambda e, src=src, sl=sl, nrow=nrow: e.dma_start(out=xtok[0:nrow, sl, :], in_=src),
                w=[("xtok", sl)], dma=("xtok", sl))
            for k4 in range(4):
                b = bank()
                for j in range(4):
                    kc = k4 * 4 + j
                    add("pe", lambda e, b=b, j=j, kc=kc, sl=sl, nrow=nrow: e.transpose(
                        out=pb[b][:, j * nrow:(j + 1) * nrow], in_=xtok[0:nrow, sl, kc * 128:(kc + 1) * 128],
                        identity=cf("ident")[0:nrow, 0:nrow]),
                        r=[("xtok", sl), "cF"], w=[PS(b)])
                wk = [(dstname, k4 * 4 + j, ti) for j in range(4) for ti in tiles_of(col, col + nrow)]
                add("act", lambda e, b=b, k4=k4, col=col, nrow=nrow: e.activation(
                    out=dst[:, k4 * 4:(k4 + 1) * 4, col:col + nrow],
                    in_=pb[b][:, 0:4 * nrow].rearrange("p (a b) -> p a b", b=nrow), func=AF.Copy),
                    r=[PS(b)], w=wk)
            col += nrow

    with ExitStack() as st0:
        srcs = [(xp[i * 128:(i + 1) * 128, :], 128) for i in range(8)] + [(xs[:, :], NB)]
        load_tokens(st0, srcs, xT, "xT", None)
        S.barrier()
    dump("xT0", xT, [("xT", kc, ti) for kc in range(KC) for ti in range(3)])
    if stage < 1:
        S.finish()
        return dumps

    def norm(src, srcname, dst, dstname, gidx, tiles, nkc=KC):
        for ti, (t0, tn) in enumerate(tiles):
            b = bank()
            for k4 in range(nkc // 4):
                sl = C.cnt % 2
                C.cnt += 1
                add("act", lambda e, sl=sl, k4=k4, t0=t0, tn=tn: e.activation(
                    out=sqb[:, sl, :, 0:tn], in_=src[:, k4 * 4:(k4 + 1) * 4, t0:t0 + tn], func=AF.Square),
                    r=[(srcname, k4 * 4 + j, ti) for j in range(4)], w=[("sqb", sl)])
                for j in range(4):
                    kc = k4 * 4 + j
                    add("pe", lambda e, b=b, sl=sl, j=j, kc=kc, tn=tn: e.matmul(
                        pb[b][:, 0:tn], lhsT=cb("ones"), rhs=sqb[:, sl, j, 0:tn], start=(kc == 0), stop=(kc == nkc - 1)),
                        r=[("sqb", sl), "cB"], w=[PS(b)])
            sl = ti % 2
            add("act", lambda e, b=b, sl=sl, tn=tn: e.activation(
                out=lnb[:, sl, 0:tn], in_=pb[b][:, 0:tn], func=AF.Ln, scale=1.0 / (nkc * 128), bias=EPS),
                r=[PS(b)], w=[("lnb", sl)])
            add("act", lambda e, b=b, sl=sl, tn=tn: e.activation(
                out=pb[b][:, 0:tn], in_=lnb[:, sl, 0:tn], func=AF.Exp, scale=-0.5),
                r=[("lnb", sl)], w=[PS(b)])
            for kc in range(nkc):
                add("dve", lambda e, b=b, kc=kc, t0=t0, tn=tn: e.scalar_tensor_tensor(
                    out=dst[:, kc, t0:t0 + tn], in0=src[:, kc, t0:t0 + tn], scalar=G(gidx, kc),
                    in1=pb[b][:, 0:tn], op0=ALU.mult, op1=ALU.mult),
                    r=[(srcname, kc, ti), PS(b), "gvec"], w=[(dstname, kc, ti)])

    def stream(wtiles, nkc, rhs, rkeys, consume, tiles=TT):
        for wap, cids in wtiles:
            slot = C.wslot
            C.wslot = (C.wslot + 1) % NSLOT
            wv = wsl[slot][:, 0:nkc * 256].rearrange("p (k n) -> p k n", n=256)
            add("pool", lambda e, wv=wv, wap=wap: e.dma_start(out=wv, in_=wap.rearrange("(kc p) n -> p kc n", p=128)),
                w=[("w", slot)], dma=("w", slot))
            for c, cid in enumerate(cids):
                banks = []
                for ti, (t0, tn) in enumerate(tiles):
                    b = bank()
                    for kc in range(nkc):
                        add("pe", lambda e, b=b, wv=wv, kc=kc, c=c, ti=ti, tn=tn: e.matmul(
                            pb[b][:, 0:tn], lhsT=wv[:, kc, c * 128:(c + 1) * 128], rhs=rhs(kc, ti),
                            start=(kc == 0), stop=(kc == nkc - 1)),
                            r=[("w", slot)] + rkeys(kc, ti), w=[PS(b)])
                    banks.append(b)
                consume(cid, banks)

    h_rhs = lambda kc, ti: hT[:, kc, TT[ti][0]:TT[ti][0] + TT[ti][1]]
    h_keys = lambda kc, ti: [("hT", kc, ti)]

    def wcols(w, l, c0, nk=KC, r0=0):
        return w[l, r0:r0 + nk * 128, c0:c0 + 256]


    with ExitStack() as stm:
        pass
    raw = sb(es, "raw", [128, 2, 352], BF16)
    rta = sb(es, "rta", [128, 2, 352], F32)
    rtb = sb(es, "rtb", [128, 2, 352], F32)

    def rotary(banks, dstf, wkeys):
        for ti, (t0, tn) in enumerate(TT):
            b = banks[ti]
            sl = C.cnt % 2
            C.cnt += 1
            b2 = bank()
            add("act", lambda e, b=b, sl=sl, tn=tn: e.activation(out=raw[:, sl, 0:tn], in_=pb[b][:, 0:tn], func=AF.Copy),
                r=[PS(b)], w=[("raw", sl)])
            add("pe", lambda e, b2=b2, sl=sl, tn=tn: e.matmul(pb[b2][:, 0:tn], lhsT=cb("pswap"), rhs=raw[:, sl, 0:tn],
                                                              start=True, stop=True),
                r=[("raw", sl), "cB"], w=[PS(b2)])
            add("dve", lambda e, b=b, sl=sl, t0=t0, tn=tn: e.tensor_tensor(
                out=rta[:, sl, 0:tn], in0=pb[b][:, 0:tn], in1=cb("cos", t0, tn), op=ALU.mult),
                r=[PS(b), "cB"], w=[("rta", sl)])
            add("dve", lambda e, b2=b2, sl=sl, t0=t0, tn=tn: e.tensor_tensor(
                out=rtb[:, sl, 0:tn], in0=pb[b2][:, 0:tn], in1=cb("sin", t0, tn), op=ALU.mult),
                r=[PS(b2), "cB"], w=[("rtb", sl)])
            add(os.environ.get("ROT_ENG", "dve"), lambda e, sl=sl, t0=t0, tn=tn, dstf=dstf: e.tensor_tensor(
                out=dstf(t0, tn), in0=rta[:, sl, 0:tn], in1=rtb[:, sl, 0:tn], op=ALU.add),
                r=[("rta", sl), ("rtb", sl)], w=[wkeys(ti)])

    def evac_copy(banks, dstf, wkeys, func=AF.Copy, eng="act"):
        for ti, (t0, tn) in enumerate(TT):
            b = banks[ti]
            add("act", lambda e, b=b, t0=t0, tn=tn, dstf=dstf, func=func: e.activation(
                out=dstf(t0, tn), in_=pb[b][:, 0:tn], func=func), r=[PS(b)], w=[wkeys(ti)])

    def resid_add(banks, kc):
        for ti, (t0, tn) in enumerate(TT):
            b = banks[ti]
            add("dve", lambda e, b=b, t0=t0, tn=tn, kc=kc: e.tensor_tensor(
                out=xT[:, kc, t0:t0 + tn], in0=pb[b][:, 0:tn], in1=xT[:, kc, t0:t0 + tn], op=ALU.add),
                r=[PS(b), ("xT", kc, ti)], w=[("xT", kc, ti)])

    ALLT = [0, 1, 2]

    for l in range(2):
        if stage < 2:
            break
        norm(xT, "xT", hT, "hT", l, TT)
        ag1_in, ag1_out, ag2_in, ag2_out = ag1_in_l[l], ag1_out_l[l], ag2_in_l[l], ag2_out_l[l]
        dump("h%d" % l, hT, [("hT", kc, ti) for kc in range(KC) for ti in ALLT])
        with ExitStack() as sm:
            AB = sb(sm, "AB", [128, 8, T], BF16)
            sA = ExitStack()
            skT = sb(sA, "skT", [128, 2, T], BF16)
            svtok = sb(sA, "svtok", [128, 8, 256], BF16)
            vT = sb(sA, "vT", [128, 2, T], BF16)
            hal = sb(sA, "hal", [128, 512], F32)
            smp = sb(sA, "smp", [128, 2, 2, NB], F32)
            def c_sk(cid, banks):
                kh = cid
                evac_copy(banks, lambda t0, tn: skT[:, kh, t0:t0 + tn], lambda ti: ("skT", kh, ti))
                b = banks[2]
                add("act", lambda e: e.activation(out=hal[:, kh * 128:(kh + 1) * 128], in_=pb[b][:, 192:320], func=AF.Copy),
                    r=[PS(b)], w=[("hal", kh)])
                add("act", lambda e: e.activation(out=smp[:, 0, kh, :], in_=pb[b][:, 320:336], func=AF.Copy),
                    r=[PS(b)], w=[("smp", 0, kh)])

            def c_sv(cid, banks):
                kh = cid
                evac_copy(banks, lambda t0, tn: vT[:, kh, t0:t0 + tn], lambda ti: ("vT", kh, ti))
                b = banks[2]
                add("act", lambda e: e.activation(out=smp[:, 1, kh, :], in_=pb[b][:, 320:336], func=AF.Copy),
                    r=[PS(b)], w=[("smp", 1, kh)])
                add("act", lambda e: e.activation(out=rta[:, 0, 0:128], in_=pb[b][:, 192:320], func=AF.Copy),
                    r=[PS(b)], w=[("rta", 0)])
                b3 = bank()
                add("pe", lambda e: e.transpose(out=pb[b3][:, 0:128], in_=rta[:, 0, 0:128], identity=cf("ident")),
                    r=[("rta", 0), "cF"], w=[PS(b3)])
                add("act", lambda e: e.activation(out=hal[:, 256 + kh * 128:256 + (kh + 1) * 128], in_=pb[b3][:, 0:128], func=AF.Copy),
                    r=[PS(b3)], w=[("hal", 2 + kh)])
                b4 = bank()
                pv = pb[b4][:].bitcast(BF16)
                for n in range(8):
                    add("pe", lambda e, n=n: e.transpose(out=pv[:, n * 128:(n + 1) * 128], in_=vT[:, kh, n * 128:(n + 1) * 128],
                                                         identity=cb("ident")),
                        r=[("vT", kh, ti) for ti in ALLT] + ["cB"], w=[PS(b4)])
                add("dve", lambda e: e.tensor_copy(out=svtok[:, :, kh * 128:(kh + 1) * 128],
                                                   in_=pv[:, 0:1024].rearrange("p (a b) -> p a b", b=128)),
                    r=[PS(b4)], w=[("svtok", kh)])

            stream([(wcols(w_in, l, 5120), [0, 1])], KC, h_rhs, h_keys, c_sk)
            stream([(wcols(w_in, l, 5376), [0, 1])], KC, h_rhs, h_keys, c_sv)
            HK = [("hal", i) for i in range(4)]
            add("sp", lambda e: e.dma_start(out=ag1_in.ap(), in_=hal[:]), r=HK, w=["ag1_in"], dma=("ag1i", l))
            add("sp", lambda e: e.dma_start(out=wvp[l], in_=hal[:, 256:512]), r=HK, dma="o_hal")
            bk = bank()
            for kh in range(2):
                add("pe", lambda e, kh=kh: e.transpose(out=pb[bk][:, kh * 128:(kh + 1) * 128], in_=hal[:, kh * 128:(kh + 1) * 128],
                                                       identity=cf("ident")), r=HK + ["cF"], w=[PS(bk)])
            ktk = sb(sA, "ktk", [128, 256], F32)
            add("act", lambda e: e.activation(out=ktk[:], in_=pb[bk][:, 0:256], func=AF.Copy), r=[PS(bk)], w=["ktk"])
            add("sp", lambda e: e.dma_start(out=wkp[l], in_=ktk[:]), r=["ktk"], dma="o_ktk")
            if C.use_ag:
              add("pool", lambda e: e.collective_compute("AllGather", ALU.bypass, replica_groups=[list(range(NCORES))],
                                                       ins=[ag1_in.ap()], outs=[ag1_out.ap()]),
                r=["ag1_in"], w=["ag1_out"], dma=("ag1", l), inc=1, nobar=True)
            dump("skT%d" % l, skT, [("skT", kh, ti) for kh in range(2) for ti in ALLT])
            dump("svtok%d" % l, svtok, [("svtok", kh) for kh in range(2)])
            if stage < 3:
                S.barrier()
                sA.close()
                continue

            def blk_of(c0, c1):
                return sorted(set([min(c // 128, 8) for c in (c0, c1 - 1)] + list(range(c0 // 128, min((c1 - 1) // 128, 8) + 1))))

            def c_sq(cid, banks):
                for ti, (t0, tn) in enumerate(TT):
                    b = banks[ti]
                    add("act", lambda e, b=b, t0=t0, tn=tn: e.activation(out=AB[:, cid, t0:t0 + tn], in_=pb[b][:, 0:tn], func=AF.Copy),
                        r=[PS(b)], w=[("AB", cid, k) for k in blk_of(t0, t0 + tn)])

            stream([(wcols(w_in, l, 4096 + 256 * j), [2 * j, 2 * j + 1]) for j in range(4)], KC, h_rhs, h_keys, c_sq)
            dump("sq%d" % l, AB, [("AB", h, k) for h in range(8) for k in range(9)])

            sp1 = ExitStack()
            kTp = sb(sp1, "kTp", [128, 2, T], BF16)
            vtokp = sb(sp1, "vtokp", [128, 8, 256], BF16)
            ktok = sb(sp1, "ktok", [128, 2, 8, 128], BF16)
            Ebuf = sb(sp1, "Ebuf", [128, 8, 128], F32)

            def c_rk(cid, banks):
                c = cid % 2
                if os.environ.get("NOROT"):
                    evac_copy(banks, lambda t0, tn: kTp[:, c, t0:t0 + tn], lambda ti: ("kTp", c, ti))
                else:
                    rotary(banks, lambda t0, tn: kTp[:, c, t0:t0 + tn], lambda ti: ("kTp", c, ti))

            def v_transposes(c):
                b4 = bank()
                pv = pb[b4][:].bitcast(BF16)
                for n in range(8):
                    add("pe", lambda e, n=n: e.transpose(out=pv[:, n * 128:(n + 1) * 128], in_=vT[:, c, n * 128:(n + 1) * 128],
                                                         identity=cb("ident")),
                        r=[("vT", c, ti) for ti in ALLT] + ["cB"], w=[PS(b4)])
                add("dve", lambda e: e.tensor_copy(out=vtokp[:, :, c * 128:(c + 1) * 128],
                                                   in_=pv[:, 0:1024].rearrange("p (a b) -> p a b", b=128)),
                    r=[PS(b4)], w=[("vtokp", c)])

            def c_rv(cid, banks):
                c = cid % 2
                evac_copy(banks, lambda t0, tn: vT[:, c, t0:t0 + tn], lambda ti: ("vT", c, ti))
                v_transposes(c)

            def k_transposes(c, h, scale_ap_bcast):
                b5 = bank()
                pk = pb[b5][:].bitcast(BF16)
                for n in range(8):
                    add("pe", lambda e, n=n: e.transpose(out=pk[:, n * 128:(n + 1) * 128], in_=kTp[:, c, n * 128:(n + 1) * 128],
                                                         identity=cb("ident")),
                        r=[("kTp", c, ti) for ti in ALLT] + ["cB"], w=[PS(b5)])
                add("dve", lambda e: e.tensor_tensor(out=ktok[:, c], in0=pk[:, 0:1024].rearrange("p (a b) -> p a b", b=128),
                                                     in1=scale_ap_bcast, op=ALU.mult),
                    r=[PS(b5), "cF"], w=[("ktok", c)])

            for j in range(int(os.environ.get('NPAIR', 4))):
                stream([(wcols(w_in, l, 1024 + 256 * j), [2 * j, 2 * j + 1])], KC, h_rhs, h_keys, c_rk)
                stream([(wcols(w_in, l, 2048 + 256 * j), [2 * j, 2 * j + 1])], KC, h_rhs, h_keys, c_rv)
                for c in range(2 if not os.environ.get('NO_KT') else 0):
                    h = 2 * j + c
                    k_transposes(c, h, cf("zetaE", h * 8, 8).unsqueeze(2).broadcast_to([128, 8, 128]))
                    b6 = bank()
                    for n in range(8):
                        add("pe", lambda e, n=n, c=c, b6=b6: e.matmul(pb[b6][:, 0:128], lhsT=ktok[:, c, n, :],
                                                                      rhs=vtokp[:, n, c * 128:(c + 1) * 128], start=(n == 0), stop=(n == 7)),
                            r=[("ktok", c), ("vtokp", c)], w=[PS(b6)])
                    add("act", lambda e, h=h, b6=b6: e.activation(out=Ebuf[:, h, :], in_=pb[b6][:, 0:128], func=AF.Copy),
                        r=[PS(b6)], w=[("Ebuf", h)])
            EK = [("Ebuf", h) for h in range(8)]
            add("sp", lambda e: e.dma_start(out=ag2_in.ap(), in_=Ebuf[:].rearrange("p a b -> p (a b)")), r=EK, w=["ag2_in"], dma=("ag2i", l))
            if C.use_ag:
                add("pool", lambda e: e.collective_compute("AllGather", ALU.bypass, replica_groups=[list(range(NCORES))],
                                                           ins=[ag2_in.ap()], outs=[ag2_out.ap()]),
                    r=["ag2_in"], w=["ag2_out"], dma=("ag2", l), inc=1, nobar=True)
            dump("Ebuf%d" % l, Ebuf, EK)
            S.barrier()
            sp1.close()
            if stage < 4:
                sA.close()
                continue


            with ExitStack() as ss:
                hacc = sb(ss, "hacc", [128, 512], F32)
                hld = sb(ss, "hld", [128, 2, 512], F32)
                skh = sb(ss, "skh", [128, 2, 128], BF16)
                svh = sb(ss, "svh", [128, 256], BF16)
                esk = sb(ss, "esk", [128, 8], F32)
                pex = sb(ss, "pex", [128, 2, 2, 512], BF16)
                pmk = sb(ss, "pmk", [128, 2, 2, 512], BF16)
                dtt = sb(ss, "dtt", [128, 2, 512], F32)
                add("act", lambda e: e.activation(out=esk[:], in_=gvec[:, 160 + 8 * l:168 + 8 * l], func=AF.Exp),
                    r=["gvec"], w=["esk"])
                add("dve", lambda e: e.memset(hacc[:], 0.0), w=["hacc"])
                if C.use_ag:
                    for cp in range(8):
                        sl = cp % 2
                        add("sp", lambda e, cp=cp, sl=sl: e.dma_start(out=hld[:, sl, :], in_=ag1_out[cp * 128:(cp + 1) * 128, :]),
                            r=["ag1_out"], w=[("hld", sl)], dma=("hld", sl))
                        add("dve", lambda e, cp=cp, sl=sl: e.scalar_tensor_tensor(
                            out=hacc[:], in0=hld[:, sl, :], scalar=cf("sel", cp, 1), in1=hacc[:], op0=ALU.mult, op1=ALU.add),
                            r=[("hld", sl), "hacc", "cF"], w=["hacc"])
                add("act", lambda e: e.activation(out=skh[:].rearrange("p a b -> p (a b)"), in_=hacc[:, 0:256], func=AF.Copy),
                    r=["hacc"], w=["skh"])
                add("act", lambda e: e.activation(out=svh[:], in_=hacc[:, 256:512], func=AF.Copy), r=["hacc"], w=["svh"])
                SKK = lambda kh: [("skT", kh, ti) for ti in ALLT]
                for n in range(8):
                    for kh in range(2):
                        sl = C.cnt % 2
                        C.cnt += 1
                        qk = [("AB", 4 * kh + g, n) for g in range(4)]
                        q_ap = lambda n=n, kh=kh: AB[:, 4 * kh:4 * kh + 4, n * 128:(n + 1) * 128]
                        bd = bank()
                        bo = bank()
                        for j in range(2):
                            if j == 0 and n == 0:
                                kap = lambda kh=kh: skh[:, kh, :]
                                vap = lambda kh=kh: svh[:, kh * 128:(kh + 1) * 128]
                                kk, vk = ["skh"], ["svh"]
                                mk = "mprev0"
                            else:
                                nn = n - 1 + j
                                kap = lambda kh=kh, nn=nn: skT[:, kh, nn * 128:(nn + 1) * 128]
                                vap = lambda kh=kh, nn=nn: svtok[:, nn, kh * 128:(kh + 1) * 128]
                                kk, vk = SKK(kh), [("svtok", kh)]
                                mk = "mprev" if j == 0 else "mcur"
                            bs = bank()
                            add("pe", lambda e, bs=bs, kap=kap, q_ap=q_ap: e.matmul(pb[bs][:, 0:512], lhsT=kap(), rhs=q_ap(), start=True, stop=True),
                                r=kk + qk, w=[PS(bs)])
                            add("act", lambda e, bs=bs, sl=sl, j=j: e.activation(out=pex[:, sl, j, :], in_=pb[bs][:, 0:512], func=AF.Exp, scale=SC),
                                r=[PS(bs)], w=[("pex", sl, j)])
                            add("dve", lambda e, sl=sl, j=j, mk=mk: e.tensor_tensor(
                                out=pmk[:, sl, j, :].rearrange("p (a b) -> p a b", b=128),
                                in0=pex[:, sl, j, :].rearrange("p (a b) -> p a b", b=128),
                                in1=cb(mk).unsqueeze(1).broadcast_to([128, 4, 128]), op=ALU.mult),
                                r=[("pex", sl, j), "cB"], w=[("pmk", sl, j)])
                            add("pe", lambda e, bd=bd, sl=sl, j=j: e.matmul(pb[bd][:, 0:512], lhsT=cb("ones"), rhs=pmk[:, sl, j, :], start=(j == 0), stop=(j == 1)),
                                r=[("pmk", sl, j), "cB"], w=[PS(bd)])
                            add("pe", lambda e, bo=bo, sl=sl, j=j, vap=vap: e.matmul(pb[bo][:, 0:512], lhsT=vap(), rhs=pmk[:, sl, j, :], start=(j == 0), stop=(j == 1)),
                                r=[("pmk", sl, j)] + vk, w=[PS(bo)])
                        add("dve", lambda e, bd=bd, sl=sl, kh=kh: e.tensor_tensor(
                            out=dtt[:, sl, :].rearrange("p (a b) -> p a b", b=128),
                            in0=pb[bd][:, 0:512].rearrange("p (a b) -> p a b", b=128),
                            in1=esk[:, 4 * kh:4 * kh + 4].unsqueeze(2).broadcast_to([128, 4, 128]), op=ALU.add),
                            r=[PS(bd), "esk"], w=[("dtt", sl)])
                        add("act", lambda e, sl=sl: e.activation(out=dtt[:, sl, :], in_=dtt[:, sl, :], func=AF.Ln), r=[("dtt", sl)], w=[("dtt", sl)])
                        add("act", lambda e, sl=sl: e.activation(out=dtt[:, sl, :], in_=dtt[:, sl, :], func=AF.Exp, scale=-1.0), r=[("dtt", sl)], w=[("dtt", sl)])
                        add("dve", lambda e, bo=bo, sl=sl, n=n, kh=kh: e.tensor_tensor(
                            out=AB[:, 4 * kh:4 * kh + 4, n * 128:(n + 1) * 128],
                            in0=pb[bo][:, 0:512].rearrange("p (a b) -> p a b", b=128),
                            in1=dtt[:, sl, :].rearrange("p (a b) -> p a b", b=128), op=ALU.mult),
                            r=[PS(bo), ("dtt", sl)], w=qk)
                dump("so%d" % l, AB, [("AB", h, k) for h in range(8) for k in range(9)])


                if stage >= 5:
                    kcg = sb(ss, "kcg", [128, 2, 4, 256], BF16)
                    vcg = sb(ss, "vcg", [128, 2, 4, 256], BF16)
                    kcT = sb(ss, "kcT", [128, 2, 8, 128], BF16)
                    pw = sb(ss, "pw", [128, 128], BF16)
                    prod = sb(ss, "prod", [128, 128], BF16)
                    pn = sb(ss, "pn", [128, 128], F32)
                    t1 = sb(ss, "t1", [128, 128], F32)
                    dn = sb(ss, "dn", [128, 128], F32)
                    nkv = sb(ss, "nkv", [128, 2, 256], F32)
                    bov = hold()
                    bdn = hold()
                    QS = [("AB", h, 8) for h in range(8)]
                    for gi in range(4):
                        sl = gi % 2
                        add("pool", lambda e, gi=gi, sl=sl: e.dma_start(out=kcg[:, sl], in_=cwk[l, 4 * gi:4 * gi + 4].rearrange("b s f -> s b f")),
                            w=[("kcg", sl)], dma=("kcg", sl))
                        add("pool", lambda e, gi=gi, sl=sl: e.dma_start(out=vcg[:, sl], in_=cwv[l, 4 * gi:4 * gi + 4].rearrange("b s f -> s b f")),
                            w=[("vcg", sl)], dma=("vcg", sl))
                        bt = bank()
                        pt = pb[bt][:].bitcast(BF16)
                        for bi in range(4):
                            for kh in range(2):
                                add("pe", lambda e, bi=bi, kh=kh, sl=sl, pt=pt: e.transpose(
                                    out=pt[:, (bi * 2 + kh) * 128:(bi * 2 + kh + 1) * 128], in_=kcg[:, sl, bi, kh * 128:(kh + 1) * 128],
                                    identity=cb("ident")), r=[("kcg", sl), "cB"], w=[PS(bt)])
                        add("act", lambda e, sl=sl, pt=pt: e.activation(out=kcT[:, sl].rearrange("p a b -> p (a b)"), in_=pt[:, 0:1024], func=AF.Copy),
                            r=[PS(bt)], w=[("kcT", sl)])
                        bsc = bank()
                        for bi in range(4):
                            for kh in range(2):
                                b_ = 4 * gi + bi
                                add("pe", lambda e, bi=bi, kh=kh, sl=sl, b_=b_, bsc=bsc: e.matmul(
                                    pb[bsc][:, bi * 8 + kh * 4:bi * 8 + kh * 4 + 4], lhsT=kcT[:, sl, bi * 2 + kh, :],
                                    rhs=AB[:, 4 * kh:4 * kh + 4, 1024 + b_:1025 + b_], start=True, stop=True),
                                    r=[("kcT", sl)] + QS, w=[PS(bsc)])
                        add("act", lambda e, gi=gi, bsc=bsc: e.activation(out=pw[:, 32 * gi:32 * gi + 32], in_=pb[bsc][:, 0:32], func=AF.Exp, scale=SC),
                            r=[PS(bsc)], w=[("pw", gi)])
                        for bi in range(4):
                            for kh in range(2):
                                col = 32 * gi + bi * 8 + kh * 4
                                add("pe", lambda e, bi=bi, kh=kh, sl=sl, col=col: e.matmul(
                                    pb[bov][:, col:col + 4], lhsT=vcg[:, sl, bi, kh * 128:(kh + 1) * 128], rhs=pw[:, col:col + 4],
                                    start=True, stop=True), r=[("vcg", sl), ("pw", gi)], w=[PS(bov)])
                        add("pe", lambda e, gi=gi: e.matmul(pb[bdn][:, 32 * gi:32 * gi + 32], lhsT=cb("ones"), rhs=pw[:, 32 * gi:32 * gi + 32],
                                                            start=True, stop=True), r=[("pw", gi), "cB"], w=[PS(bdn)])
                    SMP = [("smp", a, kh) for a in range(2) for kh in range(2)]
                    add("dve", lambda e: e.tensor_tensor(
                        out=prod[:].rearrange("p (b k g) -> p b k g", k=2, g=4),
                        in0=bass.AP(tensor=AB, offset=1024, ap=[[8 * T, 128], [1, 16], [4 * T, 2], [T, 4]]),
                        in1=bass.AP(tensor=smp, offset=0, ap=[[64, 128], [1, 16], [16, 2], [0, 4]]), op=ALU.mult),
                        r=QS + SMP, w=["prod"])
                    bsn = bank()
                    add("pe", lambda e: e.matmul(pb[bsn][:, 0:128], lhsT=cb("ones"), rhs=prod[:], start=True, stop=True),
                        r=["prod", "cB"], w=[PS(bsn)])
                    add("act", lambda e: e.activation(out=pn[:], in_=pb[bsn][:, 0:128], func=AF.Exp, scale=SC), r=[PS(bsn)], w=["pn"])
                    add("dve", lambda e: e.tensor_tensor(
                        out=t1[:].rearrange("p (b k g) -> p b k g", k=2, g=4), in0=pn[:].rearrange("p (b k g) -> p b k g", k=2, g=4),
                        in1=bass.AP(tensor=smp, offset=32, ap=[[64, 128], [1, 16], [16, 2], [0, 4]]), op=ALU.mult),
                        r=["pn"] + SMP, w=["t1"])
                    add("dve", lambda e: e.tensor_tensor(out=t1[:], in0=pb[bov][:, 0:128], in1=t1[:], op=ALU.add), r=[PS(bov), "t1"], w=["t1"])
                    add("dve", lambda e: e.tensor_tensor(out=dn[:], in0=pb[bdn][:, 0:128], in1=pn[:], op=ALU.add), r=[PS(bdn), "pn"], w=["dn"])
                    add("dve", lambda e: e.tensor_tensor(
                        out=dn[:].rearrange("p (b h) -> p b h", h=8), in0=dn[:].rearrange("p (b h) -> p b h", h=8),
                        in1=esk[:].unsqueeze(1).broadcast_to([128, 16, 8]), op=ALU.add), r=["dn", "esk"], w=["dn"])
                    add("act", lambda e: e.activation(out=dn[:], in_=dn[:], func=AF.Ln), r=["dn"], w=["dn"])
                    add("act", lambda e: e.activation(out=dn[:], in_=dn[:], func=AF.Exp, scale=-1.0), r=["dn"], w=["dn"])
                    add("dve", lambda e: e.tensor_tensor(
                        out=bass.AP(tensor=AB, offset=1024, ap=[[8 * T, 128], [1, 16], [T, 8]]),
                        in0=t1[:].rearrange("p (b h) -> p b h", h=8), in1=dn[:].rearrange("p (b h) -> p b h", h=8), op=ALU.mult),
                        r=["t1", "dn"], w=QS)
                    release(bov)
                    release(bdn)
                    add("sp", lambda e: e.dma_start(out=wks[l, :, 0:127, :], in_=cwk[l, :, 1:128, :]), dma="o_wk")
                    add("sp", lambda e: e.dma_start(out=wvs[l, :, 0:127, :], in_=cwv[l, :, 1:128, :]), dma="o_wv")
                    bt2 = bank()
                    for a in range(2):
                        for kh in range(2):
                            add("pe", lambda e, a=a, kh=kh: e.transpose(out=pb[bt2][0:16, a * 256 + kh * 128:a * 256 + (kh + 1) * 128],
                                                                      in_=smp[:, a, kh, :], identity=cf("ident")),
                                r=SMP + ["cF"], w=[PS(bt2)])
                    add("act", lambda e: e.activation(out=nkv[0:16].rearrange("p a b -> p (a b)"), in_=pb[bt2][0:16, 0:512], func=AF.Copy),
                        r=[PS(bt2)], w=["nkv"])
                    add("sp", lambda e: e.dma_start(out=wks[l, :, 127, :], in_=nkv[0:16, 0, :]), r=["nkv"], dma="o_nk")
                    add("sp", lambda e: e.dma_start(out=wvs[l, :, 127, :], in_=nkv[0:16, 1, :]), r=["nkv"], dma="o_nv")
                    dump("sos%d" % l, AB, [("AB", h, k) for h in range(8) for k in range(9)])
                    ab_rhs = lambda kc, ti: AB[:, kc, TT[ti][0]:TT[ti][0] + TT[ti][1]]
                    ab_keys = lambda kc, ti: [("AB", kc, k) for k in blk_of(TT[ti][0], TT[ti][0] + TT[ti][1])]
                    stream([(wcols(w_out, l, 256 * f, nk=8, r0=1024), [2 * f, 2 * f + 1]) for f in range(8)], 8, ab_rhs, ab_keys,
                           lambda cid, banks: resid_add(banks, cid))
            if stage < 6:
                S.barrier()
                sA.close()
                continue
            S.barrier()
            sA.close()
            rst = sb(sm, "rst", [128, 8, 128], F32)
            with ExitStack() as sg:
                gE = sb(sg, "gE", [128, 2, 1024], F32)
                rfin = sb(sg, "rfin", [128, 8, 128], F32)
                gtmp = sb(sg, "gtmp", [128, 2, 1024], F32)
                add("dve", lambda e: e.memset(rst[:], 0.0), w=["rst"])
                add("pool", lambda e: e.memset(rfin[:], 0.0), w=["rfin"])
                if C.use_ag:
                    for cp in range(8):
                        sl = cp % 2
                        add("sp", lambda e, cp=cp, sl=sl: e.dma_start(out=gE[:, sl, :], in_=ag2_out[cp * 128:(cp + 1) * 128, :]),
                            r=["ag2_out"], w=[("gE", sl)], dma=("gE", sl))
                        for nm, acc, eng, ti_, akey in (("coefS", rst, "dve", 0, "rst"), ("coefF", rfin, "pool", 1, "rfin")):
                            add(eng, lambda e, nm=nm, cp=cp, sl=sl, ti_=ti_: e.tensor_tensor(
                                out=gtmp[:, ti_, :].rearrange("p (a b) -> p a b", b=128), in0=gE[:, sl, :].rearrange("p (a b) -> p a b", b=128),
                                in1=cf(nm, cp * 8, 8).unsqueeze(2).broadcast_to([128, 8, 128]), op=ALU.mult),
                                r=[("gE", sl), "cF"], w=[("gtmp", ti_)])
                            add(eng, lambda e, acc=acc, ti_=ti_: e.tensor_tensor(
                                out=acc[:], in0=acc[:], in1=gtmp[:, ti_, :].rearrange("p (a b) -> p a b", b=128), op=ALU.add),
                                r=[("gtmp", ti_), akey], w=[akey])
                add("sp", lambda e: e.dma_start(out=retp[l].rearrange("h d v -> d h v"), in_=rfin[:]), r=["rfin"], dma="o_rf")
                S.barrier()
            for j in range(4):
                with ExitStack() as sp2:
                    kTp = sb(sp2, "kTp", [128, 2, T], BF16)
                    vT = sb(sp2, "vT2", [128, 2, T], BF16)
                    vtokp = sb(sp2, "vtokp", [128, 8, 256], BF16)
                    vts = sb(sp2, "vts", [128, 256], BF16)
                    kts = sb(sp2, "kts", [128, 256], BF16)
                    ktok = sb(sp2, "ktok", [128, 2, 8, 128], BF16)
                    qT = sb(sp2, "qT", [128, 2, T], BF16)
                    qx = sb(sp2, "qx", [128, 2, 1024], BF16)
                    gT = sb(sp2, "gT", [128, 2, T], BF16)
                    sTm = sb(sp2, "sTm", [128, 2, 512], BF16)
                    Rf = sb(sp2, "Rf", [128, 2, 128], F32)
                    Rb = sb(sp2, "Rb", [128, 2, 2, 128], BF16)
                    Sst = sb(sp2, "Sst", [128, 8, 128], F32)
                    Snb = sb(sp2, "Snb", [128, 8, 128], BF16)
                    kmask = sb(sp2, "kmask", [128, 8, 128], BF16)
                    add("dve", lambda e: e.memset(vts[:], 0.0), w=["vts"])
                    add("dve", lambda e: e.memset(kts[:], 0.0), w=["kts"])

                    def c_rk2(cid, banks):
                        c = cid % 2
                        rotary(banks, lambda t0, tn: kTp[:, c, t0:t0 + tn], lambda ti: ("kTp", c, ti))

                    def c_rv2(cid, banks):
                        c = cid % 2
                        evac_copy(banks, lambda t0, tn: vT[:, c, t0:t0 + tn], lambda ti: ("vT", c, ti))
                        b4 = bank()
                        pv = pb[b4][:].bitcast(BF16)
                        for n in range(8):
                            add("pe", lambda e, n=n: e.transpose(out=pv[:, n * 128:(n + 1) * 128], in_=vT[:, c, n * 128:(n + 1) * 128],
                                                                 identity=cb("ident")),
                                r=[("vT", c, ti) for ti in ALLT] + ["cB"], w=[PS(b4)])
                        add("dve", lambda e: e.tensor_copy(out=vtokp[:, :, c * 128:(c + 1) * 128],
                                                           in_=pv[:, 0:1024].rearrange("p (a b) -> p a b", b=128)),
                            r=[PS(b4)], w=[("vtokp", c)])
                        b5 = bank()
                        pv5 = pb[b5][:].bitcast(BF16)
                        add("pe", lambda e: e.transpose(out=pv5[0:16, 0:128], in_=vT[:, c, 1024:1040], identity=cb("ident")),
                            r=[("vT", c, 2), "cB"], w=[PS(b5)])
                        add("act", lambda e: e.activation(out=vts[0:16, c * 128:(c + 1) * 128], in_=pv5[0:16, 0:128], func=AF.Copy),
                            r=[PS(b5), "vts"], w=["vts"])

                    def c_rq2(cid, banks):
                        c = cid % 2
                        h = cid
                        rotary(banks, lambda t0, tn: qT[:, c, t0:t0 + tn], lambda ti: ("qT", c, ti))
                        add("dve", lambda e: e.tensor_tensor(
                            out=qx[:, c, :].rearrange("p (a b) -> p a b", b=128), in0=qT[:, c, 0:1024].rearrange("p (a b) -> p a b", b=128),
                            in1=cb("xi", h * 128, 128).unsqueeze(1).broadcast_to([128, 8, 128]), op=ALU.mult),
                            r=[("qT", c, ti) for ti in ALLT] + ["cB"], w=[("qx", c)])

                    def c_rg2(cid, banks):
                        c = cid % 2
                        evac_copy(banks, lambda t0, tn: gT[:, c, t0:t0 + tn], lambda ti: ("gT", c, ti), func=AF.Silu)

                    stream([(wcols(w_in, l, 1024 + 256 * j), [2 * j, 2 * j + 1])], KC, h_rhs, h_keys, c_rk2)
                    stream([(wcols(w_in, l, 2048 + 256 * j), [2 * j, 2 * j + 1])], KC, h_rhs, h_keys, c_rv2)
                    stream([(wcols(w_in, l, 256 * j), [2 * j, 2 * j + 1])], KC, h_rhs, h_keys, c_rq2)
                    stream([(wcols(w_in, l, 3072 + 256 * j), [2 * j, 2 * j + 1])], KC, h_rhs, h_keys, c_rg2)
                    for c in range(2):
                        h = 2 * j + c
                        KT = [("kTp", c, ti) for ti in ALLT]
                        QT = [("qT", c, ti) for ti in ALLT]
                        add("act", lambda e, c=c, h=h: e.activation(out=Rf[:, c, :], in_=rst[:, h, :], func=AF.Copy), r=["rst"], w=[("Rf", c)])
                        b5 = bank()
                        pk = pb[b5][:].bitcast(BF16)
                        for n in range(8):
                            add("pe", lambda e, n=n, c=c, pk=pk: e.transpose(out=pk[:, n * 128:(n + 1) * 128], in_=kTp[:, c, n * 128:(n + 1) * 128],
                                                                             identity=cb("ident")), r=KT + ["cB"], w=[PS(b5)])
                        add("dve", lambda e, c=c, h=h, pk=pk: e.tensor_scalar(
                            out=ktok[:, c].rearrange("p a b -> p (a b)"), in0=pk[:, 0:1024], scalar1=cf("zeta", h, 1), scalar2=None, op0=ALU.mult),
                            r=[PS(b5), "cF"], w=[("ktok", c)])
                        b7 = bank()
                        pk7 = pb[b7][:].bitcast(BF16)
                        add("pe", lambda e, c=c, pk7=pk7: e.transpose(out=pk7[0:16, 0:128], in_=kTp[:, c, 1024:1040], identity=cb("ident")),
                            r=[("kTp", c, 2), "cB"], w=[PS(b7)])
                        add("act", lambda e, c=c, pk7=pk7: e.activation(out=kts[0:16, c * 128:(c + 1) * 128], in_=pk7[0:16, 0:128], func=AF.Copy),
                            r=[PS(b7), "kts"], w=["kts"])
                        for hf in range(2):
                            bs = bank()
                            for i in range(4):
                                n = 4 * hf + i
                                add("pe", lambda e, n=n, i=i, c=c, bs=bs: e.matmul(
                                    pb[bs][:, i * 128:(i + 1) * 128], lhsT=kTp[:, c, n * 128:(n + 1) * 128], rhs=qT[:, c, n * 128:(n + 1) * 128],
                                    start=True, stop=True), r=KT + QT, w=[PS(bs)])
                            add("dve", lambda e, hf=hf, h=h, bs=bs: e.tensor_tensor(
                                out=sTm[:, hf, :].rearrange("p (a b) -> p a b", b=128), in0=pb[bs][:, 0:512].rearrange("p (a b) -> p a b", b=128),
                                in1=cb("dmask", h * 128, 128).unsqueeze(1).broadcast_to([128, 4, 128]), op=ALU.mult),
                                r=[PS(bs), "cB"], w=[("sTm", hf)])
                        bo = [hold(), hold()]
                        for n in range(8):
                            hf, i = n // 4, n % 4
                            rs = n % 2
                            add("act", lambda e, c=c, rs=rs: e.activation(out=Rb[:, c, rs, :], in_=Rf[:, c, :], func=AF.Copy),
                                r=[("Rf", c)], w=[("Rb", c, rs)])
                            add("pe", lambda e, n=n, i=i, hf=hf, c=c: e.matmul(
                                pb[bo[hf]][:, i * 128:(i + 1) * 128], lhsT=vtokp[:, n, c * 128:(c + 1) * 128], rhs=sTm[:, hf, i * 128:(i + 1) * 128],
                                start=True, stop=False), r=[("vtokp", c), ("sTm", hf)], w=[PS(bo[hf])])
                            add("pe", lambda e, n=n, i=i, hf=hf, c=c, rs=rs: e.matmul(
                                pb[bo[hf]][:, i * 128:(i + 1) * 128], lhsT=Rb[:, c, rs, :], rhs=qx[:, c, n * 128:(n + 1) * 128],
                                start=False, stop=True), r=[("Rb", c, rs), ("qx", c)], w=[PS(bo[hf])])
                            if n < 7:
                                bkv = bank()
                                add("pe", lambda e, n=n, c=c, bkv=bkv: e.matmul(pb[bkv][:, 0:128], lhsT=ktok[:, c, n, :],
                                                                                rhs=vtokp[:, n, c * 128:(c + 1) * 128], start=True, stop=True),
                                    r=[("ktok", c), ("vtokp", c)], w=[PS(bkv)])
                                add("dve", lambda e, c=c, h=h, bkv=bkv: e.scalar_tensor_tensor(
                                    out=Rf[:, c, :], in0=Rf[:, c, :], scalar=cf("gpow", h, 1), in1=pb[bkv][:, 0:128], op0=ALU.mult, op1=ALU.add),
                                    r=[("Rf", c), PS(bkv), "cF"], w=[("Rf", c)])
                        bos = hold()
                        for hf in range(2):
                            add("sp", lambda e, hf=hf, h=h: e.dma_start(out=Sst[:], in_=sret[l, 8 * hf:8 * hf + 8, h].rearrange("b d v -> d b v")),
                                w=["Sst"], dma="Sst")
                            add("pool", lambda e, hf=hf, c=c: e.tensor_tensor(
                                out=kmask[:], in0=kts[:, c * 128:(c + 1) * 128].unsqueeze(1).broadcast_to([128, 8, 128]),
                                in1=cf("oh16", 8 * hf, 8).unsqueeze(2).broadcast_to([128, 8, 128]), op=ALU.mult),
                                r=["kts", "cF"], w=["kmask"])
                            for q4 in range(2):
                                bk = bank()
                                for i in range(4):
                                    bb = 4 * q4 + i
                                    add("pe", lambda e, bb=bb, i=i, c=c, bk=bk: e.matmul(
                                        pb[bk][:, i * 128:(i + 1) * 128], lhsT=kmask[:, bb, :], rhs=vts[:, c * 128:(c + 1) * 128], start=True, stop=True),
                                        r=["kmask", "vts"], w=[PS(bk)])
                                add("dve", lambda e, q4=q4, h=h, bk=bk: e.scalar_tensor_tensor(
                                    out=Sst[:, 4 * q4:4 * q4 + 4, :].rearrange("p a b -> p (a b)"), in0=Sst[:, 4 * q4:4 * q4 + 4, :].rearrange("p a b -> p (a b)"),
                                    scalar=cf("gamma", h, 1), in1=pb[bk][:, 0:512], op0=ALU.mult, op1=ALU.add),
                                    r=["Sst", PS(bk), "cF"], w=["Sst"])
                            add("sp", lambda e, hf=hf, h=h: e.dma_start(out=rets[l, 8 * hf:8 * hf + 8, h].rearrange("b d v -> d b v"), in_=Sst[:]),
                                r=["Sst"], dma="o_Sst")
                            add("act", lambda e: e.activation(out=Snb[:].rearrange("p a b -> p (a b)"), in_=Sst[:].rearrange("p a b -> p (a b)"), func=AF.Copy),
                                r=["Sst"], w=["Snb"])
                            for bb in range(8):
                                col = 8 * hf + bb
                                add("pe", lambda e, bb=bb, col=col, c=c: e.matmul(
                                    pb[bos][:, col:col + 1], lhsT=Snb[:, bb, :], rhs=qT[:, c, 1024 + col:1025 + col], start=True, stop=True),
                                    r=["Snb", ("qT", c, 2)], w=[PS(bos)])
                        units = [(bo[0], 0, 256, 0), (bo[0], 256, 256, 256), (bo[1], 0, 256, 512), (bo[1], 256, 256, 768), (bos, 0, 16, 1024)]
                        for (bsrc, c0, wd, dcol) in units:
                            sl = C.cnt % 2
                            C.cnt += 1
                            add("act", lambda e, bsrc=bsrc, c0=c0, wd=wd, sl=sl: e.activation(out=raw[:, sl, 0:wd], in_=pb[bsrc][:, c0:c0 + wd], func=AF.Square),
                                r=[PS(bsrc)], w=[("raw", sl)])
                            bms = bank()
                            add("pe", lambda e, bms=bms, wd=wd, sl=sl: e.matmul(pb[bms][:, 0:wd], lhsT=cb("ones"), rhs=raw[:, sl, 0:wd], start=True, stop=True),
                                r=[("raw", sl), "cB"], w=[PS(bms)])
                            add("act", lambda e, bms=bms, wd=wd, sl=sl: e.activation(out=rta[:, sl, 0:wd], in_=pb[bms][:, 0:wd], func=AF.Ln, scale=1.0 / 128, bias=EPS),
                                r=[PS(bms)], w=[("rta", sl)])
                            add("act", lambda e, wd=wd, sl=sl: e.activation(out=rta[:, sl, 0:wd], in_=rta[:, sl, 0:wd], func=AF.Exp, scale=-0.5),
                                r=[("rta", sl)], w=[("rta", sl)])
                            add("dve", lambda e, bsrc=bsrc, c0=c0, wd=wd, sl=sl, h=h: e.scalar_tensor_tensor(
                                out=rtb[:, sl, 0:wd], in0=pb[bsrc][:, c0:c0 + wd], scalar=gvec[:, 144 + 8 * l + h:145 + 8 * l + h], in1=rta[:, sl, 0:wd],
                                op0=ALU.mult, op1=ALU.mult), r=[PS(bsrc), ("rta", sl), "gvec"], w=[("rtb", sl)])
                            add("pool", lambda e, wd=wd, sl=sl, h=h, c=c, dcol=dcol: e.tensor_tensor(
                                out=AB[:, h, dcol:dcol + wd], in0=rtb[:, sl, 0:wd], in1=gT[:, c, dcol:dcol + wd], op=ALU.mult),
                                r=[("rtb", sl)] + [("gT", c, ti) for ti in ALLT], w=[("AB", h, k) for k in blk_of(dcol, dcol + wd)])
                        release(bo[0]); release(bo[1]); release(bos)
                    S.barrier()
            dump("mixr%d" % l, AB, [("AB", h, k) for h in range(8) for k in range(9)])
            stream([(wcols(w_out, l, 256 * f, nk=8, r0=0), [2 * f, 2 * f + 1]) for f in range(8)], 8, ab_rhs, ab_keys,
                   lambda cid, banks: resid_add(banks, cid))
            dump("xmix%d" % l, xT, [("xT", kc, ti) for kc in range(KC) for ti in ALLT])
            if stage < 7:
                S.barrier()
                continue


        S.barrier()
        with ExitStack() as sc:
            ssq = sb(sc, "ssq", [128, 4], F32)
            mnT = sb(sc, "mnT", [128, KC, 256], BF16)
            mkT = sb(sc, "mkT", [128, 4, 256], BF16)
            mvtok = sb(sc, "mvtok", [128, 2, 512], BF16)
            qcT = sb(sc, "qcT", [128, 4, T], BF16)
            ocT = sb(sc, "ocT", [128, 4, T], BF16)
            scm = ExitStack()
            memt = sb(scm, "memt", [128, D], F32)
            junk = sb(scm, "junk", [128, D], BF16)
            mkf = sb(scm, "mkf", [128, 4, 256], F32)
            stg = sb(scm, "stg", [128, 2, 512], F32)
            for mt in range(2):
                add("sp", lambda e, mt=mt: e.dma_start(out=memt[:], in_=mem[mt * 128:(mt + 1) * 128, :]), w=["memt"], dma="memt")
                add("act", lambda e: e.activation(out=junk[:], in_=memt[:], func=AF.Square, accum_out=ssq[:, 0:1]), r=["memt"], w=["junk", "ssq"])
                add("act", lambda e: e.activation(out=ssq[:, 1:2], in_=ssq[:, 0:1], func=AF.Ln, scale=1.0 / D, bias=EPS), r=["ssq"], w=["ssq"])
                add("act", lambda e: e.activation(out=ssq[:, 2:3], in_=ssq[:, 1:2], func=AF.Exp, scale=-0.5), r=["ssq"], w=["ssq"])
                add("dve", lambda e: e.tensor_scalar(out=memt[:], in0=memt[:], scalar1=ssq[:, 2:3], scalar2=None, op0=ALU.mult), r=["memt", "ssq"], w=["memt"])
                for k4 in range(4):
                    b = bank()
                    for jj in range(4):
                        kc = k4 * 4 + jj
                        add("pe", lambda e, b=b, jj=jj, kc=kc: e.transpose(out=pb[b][:, jj * 128:(jj + 1) * 128], in_=memt[:, kc * 128:(kc + 1) * 128],
                                                                          identity=cf("ident")), r=["memt", "cF"], w=[PS(b)])
                    for jj in range(4):
                        kc = k4 * 4 + jj
                        add("dve", lambda e, b=b, jj=jj, kc=kc, mt=mt: e.tensor_scalar(
                            out=mnT[:, kc, mt * 128:(mt + 1) * 128], in0=pb[b][:, jj * 128:(jj + 1) * 128], scalar1=G(4 + l, kc), scalar2=None, op0=ALU.mult),
                            r=[PS(b), "gvec"], w=[("mnT", kc)])
            MT1 = [(0, 256)]

            def c_mk(cid, banks):
                b = banks[0]
                add("act", lambda e: e.activation(out=mkf[:, cid, :], in_=pb[b][:, 0:256], func=AF.Copy), r=[PS(b)], w=[("mkf", cid)])
                add("act", lambda e: e.activation(out=mkT[:, cid, :], in_=pb[b][:, 0:256], func=AF.Copy), r=[PS(b)], w=[("mkT", cid)])

            stream([(wcols(wk_c, l, 256 * f), [2 * f, 2 * f + 1]) for f in range(2)], KC,
                   lambda kc, ti: mnT[:, kc, 0:256], lambda kc, ti: [("mnT", kc)], c_mk, tiles=MT1)
            for mt in range(2):
                b = bank()
                for h in range(4):
                    add("pe", lambda e, b=b, h=h, mt=mt: e.transpose(out=pb[b][:, h * 128:(h + 1) * 128], in_=mkf[:, h, mt * 128:(mt + 1) * 128],
                                                                    identity=cf("ident")), r=[("mkf", h), "cF"], w=[PS(b)])
                add("act", lambda e, b=b, mt=mt: e.activation(out=stg[:, mt, :], in_=pb[b][:, 0:512], func=AF.Copy), r=[PS(b)], w=[("stg", mt)])
                add("sp", lambda e, mt=mt: e.dma_start(out=mkp[l, mt * 128:(mt + 1) * 128, :], in_=stg[:, mt, :]), r=[("stg", mt)], dma=("o_stg", mt))
            for f in range(2):
                slot = C.wslot
                C.wslot = (C.wslot + 1) % NSLOT
                wv = wsl[slot][:, 0:KC * 256].rearrange("p (k n) -> p k n", n=256)
                add("pool", lambda e, wv=wv, f=f: e.dma_start(out=wv, in_=wcols(wv_c, l, 256 * f).rearrange("(kc p) n -> p kc n", p=128)),
                    w=[("w", slot)], dma=("w", slot))
                for mt in range(2):
                    b = bank()
                    for kc in range(KC):
                        add("pe", lambda e, b=b, wv=wv, kc=kc, mt=mt: e.matmul(pb[b][:, 0:256], lhsT=mnT[:, kc, mt * 128:(mt + 1) * 128], rhs=wv[:, kc, :],
                                                                               start=(kc == 0), stop=(kc == KC - 1)),
                            r=[("w", slot), ("mnT", kc)], w=[PS(b)])
                    add("act", lambda e, b=b, mt=mt, f=f: e.activation(out=stg[:, mt, f * 256:(f + 1) * 256], in_=pb[b][:, 0:256], func=AF.Copy),
                        r=[PS(b)], w=[("stg", mt)])
                    add("act", lambda e, b=b, mt=mt, f=f: e.activation(out=mvtok[:, mt, f * 256:(f + 1) * 256], in_=pb[b][:, 0:256], func=AF.Copy),
                        r=[PS(b)], w=[("mvtok", mt)])
                    add("sp", lambda e, mt=mt, f=f: e.dma_start(out=mvp[l, mt * 128:(mt + 1) * 128, f * 256:(f + 1) * 256], in_=stg[:, mt, f * 256:(f + 1) * 256]),
                        r=[("stg", mt)], dma=("o_stg", mt))
            S.barrier()
            scm.close()
            norm(xT, "xT", hT, "hT", 2 + l, TT)
            stream([(wcols(wq_c, l, 256 * f), [2 * f, 2 * f + 1]) for f in range(2)], KC, h_rhs, h_keys,
                   lambda cid, banks: evac_copy(banks, lambda t0, tn: qcT[:, cid, t0:t0 + tn], lambda ti: ("qcT", cid, ti)))
            with ExitStack() as sc2:
                pc = sb(sc2, "pc", [128, 2, 2, 352], BF16)
                dtc = sb(sc2, "dtc", [128, 2, 352], F32)
                PT = [(0, 352), (352, 352), (704, 320)]
                for h in range(4):
                    for ti, (t0, tn) in enumerate(PT):
                        sl = C.cnt % 2
                        C.cnt += 1
                        bd = bank()
                        bo = bank()
                        for mt in range(2):
                            bs = bank()
                            add("pe", lambda e, bs=bs, h=h, mt=mt, t0=t0, tn=tn: e.matmul(pb[bs][:, 0:tn], lhsT=mkT[:, h, mt * 128:(mt + 1) * 128],
                                                                                          rhs=qcT[:, h, t0:t0 + tn], start=True, stop=True),
                                r=[("mkT", h), ("qcT", h, ti)], w=[PS(bs)])
                            add("act", lambda e, bs=bs, sl=sl, mt=mt, tn=tn: e.activation(out=pc[:, sl, mt, 0:tn], in_=pb[bs][:, 0:tn], func=AF.Exp, scale=SC),
                                r=[PS(bs)], w=[("pc", sl, mt)])
                            add("pe", lambda e, bd=bd, sl=sl, mt=mt, tn=tn: e.matmul(pb[bd][:, 0:tn], lhsT=cb("ones"), rhs=pc[:, sl, mt, 0:tn],
                                                                                     start=(mt == 0), stop=(mt == 1)), r=[("pc", sl, mt), "cB"], w=[PS(bd)])
                            add("pe", lambda e, bo=bo, sl=sl, mt=mt, tn=tn, h=h: e.matmul(pb[bo][:, 0:tn], lhsT=mvtok[:, mt, h * 128:(h + 1) * 128],
                                                                                          rhs=pc[:, sl, mt, 0:tn], start=(mt == 0), stop=(mt == 1)),
                                r=[("pc", sl, mt), ("mvtok", mt)], w=[PS(bo)])
                        add("act", lambda e, bd=bd, sl=sl, tn=tn: e.activation(out=dtc[:, sl, 0:tn], in_=pb[bd][:, 0:tn], func=AF.Ln), r=[PS(bd)], w=[("dtc", sl)])
                        add("act", lambda e, sl=sl, tn=tn: e.activation(out=dtc[:, sl, 0:tn], in_=dtc[:, sl, 0:tn], func=AF.Exp, scale=-1.0),
                            r=[("dtc", sl)], w=[("dtc", sl)])
                        add("dve", lambda e, bo=bo, sl=sl, t0=t0, tn=tn, h=h: e.tensor_tensor(out=ocT[:, h, t0:t0 + tn], in0=pb[bo][:, 0:tn], in1=dtc[:, sl, 0:tn], op=ALU.mult),
                            r=[PS(bo), ("dtc", sl)], w=[("ocT", h, ti)])
                S.barrier()
            with ExitStack() as sc3:
                selm = sb(sc3, "selm", [128, 16, 128], BF16)
                qts = sb(sc3, "qts", [128, 512], BF16)
                kmb = sb(sc3, "kmb", [128, 2, 2, 512], F32)
                vmb = sb(sc3, "vmb", [128, 2, 2, 512], BF16)
                prd = sb(sc3, "prd", [128, 2, 512], F32)
                scs = sb(sc3, "scs", [128, 16, 2, 4], F32)
                pcs = sb(sc3, "pcs", [128, 16, 2, 4], BF16)
                dns = sb(sc3, "dns", [128, 16, 4], F32)
                add("dve", lambda e: e.tensor_copy(out=selm[:], in_=cb("oh16").unsqueeze(2).broadcast_to([128, 16, 128])), r=["cB"], w=["selm"])
                add("dve", lambda e: e.memset(qts[:], 0.0), w=["qts"])
                bq0 = bank()
                pq0 = pb[bq0][:].bitcast(BF16)
                for h in range(4):
                    add("pe", lambda e, h=h: e.transpose(out=pq0[0:16, h * 128:(h + 1) * 128], in_=qcT[:, h, 1024:1040], identity=cb("ident")),
                        r=[("qcT", h, 2), "cB"], w=[PS(bq0)])
                add("act", lambda e: e.activation(out=qts[0:16, :], in_=pq0[0:16, 0:512], func=AF.Copy), r=[PS(bq0), "qts"], w=["qts"])
                bos = hold()
                for b_ in range(NB):
                    sl = b_ % 2
                    add("sp", lambda e, b_=b_, sl=sl: e.dma_start(out=kmb[:, sl], in_=cmk[l, b_].rearrange("(mt p) f -> p mt f", p=128)),
                        w=[("kmb", sl)], dma=("kmb", sl))
                    add("pool", lambda e, b_=b_, sl=sl: e.dma_start(out=vmb[:, sl], in_=cmv[l, b_].rearrange("(mt p) f -> p mt f", p=128)),
                        w=[("vmb", sl)], dma=("vmb", sl))
                    bq = bank()
                    add("pe", lambda e, bq=bq, b_=b_: e.matmul(pb[bq][:, 0:512], lhsT=selm[:, b_, :], rhs=qts[:], start=True, stop=True),
                        r=["selm", "qts"], w=[PS(bq)])
                    for mt in range(2):
                        add("dve", lambda e, bq=bq, sl=sl, mt=mt: e.tensor_tensor(out=prd[:, mt, :], in0=pb[bq][:, 0:512], in1=kmb[:, sl, mt, :], op=ALU.mult),
                            r=[PS(bq), ("kmb", sl)], w=[("prd", mt)])
                        add("dve", lambda e, b_=b_, mt=mt: e.tensor_reduce(out=scs[:, b_, mt, :], in_=prd[:, mt, :].rearrange("p (h d) -> p h d", d=128),
                                                                          axis=mybir.AxisListType.X, op=ALU.add),
                            r=[("prd", mt)], w=[("scs", b_)])
                    add("act", lambda e, b_=b_: e.activation(out=pcs[:, b_].rearrange("p a b -> p (a b)"), in_=scs[:, b_].rearrange("p a b -> p (a b)"),
                                                             func=AF.Exp, scale=SC), r=[("scs", b_)], w=[("pcs", b_)])
                    for h in range(4):
                        for mt in range(2):
                            add("pe", lambda e, b_=b_, h=h, mt=mt, sl=sl: e.matmul(pb[bos][:, b_ * 4 + h:b_ * 4 + h + 1], lhsT=vmb[:, sl, mt, h * 128:(h + 1) * 128],
                                                                                   rhs=pcs[:, b_, mt, h:h + 1], start=(mt == 0), stop=(mt == 1)),
                                r=[("vmb", sl), ("pcs", b_)], w=[PS(bos)])
                bdn = bank()
                PCS = [("pcs", b_) for b_ in range(NB)]
                add("pe", lambda e: e.matmul(pb[bdn][:, 0:128], lhsT=cb("ones"), rhs=pcs[:].rearrange("p a b c -> p (a b c)"), start=True, stop=True),
                    r=PCS + ["cB"], w=[PS(bdn)])
                add("act", lambda e: e.activation(out=prd[:, 0, 0:128], in_=pb[bdn][:, 0:128], func=AF.Copy), r=[PS(bdn)], w=[("prd", 0)])
                pv4 = lambda: prd[:, 0, 0:128].rearrange("p (b m h) -> p b m h", m=2, h=4)
                add("dve", lambda e: e.tensor_tensor(out=dns[:], in0=pv4()[:, :, 0, :], in1=pv4()[:, :, 1, :], op=ALU.add), r=[("prd", 0)], w=["dns"])
                add("act", lambda e: e.activation(out=dns[:], in_=dns[:], func=AF.Ln), r=["dns"], w=["dns"])
                add("act", lambda e: e.activation(out=dns[:], in_=dns[:], func=AF.Exp, scale=-1.0), r=["dns"], w=["dns"])
                add("dve", lambda e: e.tensor_tensor(out=bass.AP(tensor=ocT, offset=1024, ap=[[4 * T, 128], [1, 16], [T, 4]]),
                                                     in0=pb[bos][:, 0:64].rearrange("p (b h) -> p b h", h=4), in1=dns[:], op=ALU.mult),
                    r=[PS(bos), "dns"], w=[("ocT", h, 2) for h in range(4)])
                release(bos)
            dump("ocT%d" % l, ocT, [("ocT", h, ti) for h in range(4) for ti in ALLT])
            stream([(wcols(wo_c, l, 256 * f, nk=4), [2 * f, 2 * f + 1]) for f in range(8)], 4,
                   lambda kc, ti: ocT[:, kc, TT[ti][0]:TT[ti][0] + TT[ti][1]], lambda kc, ti: [("ocT", kc, ti)],
                   lambda cid, banks: resid_add(banks, cid))
            dump("xcross%d" % l, xT, [("xT", kc, ti) for kc in range(KC) for ti in ALLT])
            S.barrier()
        if stage < 8:
            continue


        norm(xT, "xT", hT, "hT", 6 + l, TT)
        with ExitStack() as sf:
            actb = sb(sf, "actb", [128, 12, T], BF16)
            for (c0, c1) in ((0, 12), (12, 24), (24, 34), (34, 44)):
                for cc in range(c0, c1, 2):
                    def c_gate(cid, banks, c0=c0):
                        evac_copy(banks, lambda t0, tn: actb[:, cid - c0, t0:t0 + tn], lambda ti: ("actb", cid - c0, ti), func=AF.Silu)

                    def c_up(cid, banks, c0=c0):
                        for ti, (t0, tn) in enumerate(TT):
                            b = banks[ti]
                            add("dve", lambda e, b=b, t0=t0, tn=tn, k=cid - c0: e.tensor_tensor(
                                out=actb[:, k, t0:t0 + tn], in0=pb[b][:, 0:tn], in1=actb[:, k, t0:t0 + tn], op=ALU.mult),
                                r=[PS(b), ("actb", cid - c0, ti)], w=[("actb", cid - c0, ti)])

                    stream([(wcols(w_gate, l, 128 * cc), [cc, cc + 1])], KC, h_rhs, h_keys, c_gate)
                    stream([(wcols(w_up, l, 128 * cc), [cc, cc + 1])], KC, h_rhs, h_keys, c_up)
                nk = c1 - c0
                stream([(wcols(w_down, l, 256 * f, nk=nk, r0=c0 * 128), [2 * f, 2 * f + 1]) for f in range(8)], nk,
                       lambda kc, ti: actb[:, kc, TT[ti][0]:TT[ti][0] + TT[ti][1]], lambda kc, ti: [("actb", kc, ti)],
                       lambda cid, banks: resid_add(banks, cid))
            dump("xffn%d" % l, xT, [("xT", kc, ti) for kc in range(KC) for ti in ALLT])
            S.barrier()

    if stage >= 9:
        norm(xT, "xT", xT, "xT", 8, TT)
        with ExitStack() as sy:
            ytok = sb(sy, "ytok", [128, 2, D], F32)
            for i in range(9):
                nrow = 128 if i < 8 else NB
                col = i * 128
                sl = i % 2
                XK = [("xT", kc, ti) for kc in range(KC) for ti in tiles_of(col, col + nrow)]
                for k4 in range(4):
                    b = bank()
                    for jj in range(4):
                        kc = k4 * 4 + jj
                        add("pe", lambda e, b=b, jj=jj, kc=kc, col=col, nrow=nrow: e.transpose(
                            out=pb[b][0:nrow, jj * 128:(jj + 1) * 128], in_=xT[:, kc, col:col + nrow], identity=cf("ident")),
                            r=XK + ["cF"], w=[PS(b)])
                    add("act", lambda e, b=b, k4=k4, sl=sl, nrow=nrow: e.activation(out=ytok[0:nrow, sl, k4 * 512:(k4 + 1) * 512], in_=pb[b][0:nrow, 0:512], func=AF.Copy),
                        r=[PS(b)], w=[("ytok", sl, k4)])
                dst = yp[col:col + 128, :] if i < 8 else ys[:, :]
                add("sp", lambda e, dst=dst, sl=sl, nrow=nrow: e.dma_start(out=dst, in_=ytok[0:nrow, sl, :]),
                    r=[("ytok", sl, k4) for k4 in range(4)], dma=("o_y", sl))

    S.finish()
    return dumps


def _finish_build(S, dumps):
    S.finish()
    return dumps


def make_program(stage=99, dbg=()):
    nc0 = bass.Bass("TRN2", target_bir_lowering=False)
    S0 = Sched(nc0)
    build(nc0, S0, stage, dbg)
    nc = bass.Bass("TRN2", target_bir_lowering=False)
    S = Sched(nc, signaled=S0.signaled)
    dumps = build(nc, S, stage, dbg)
    assert S.n == S0.n, (S.n, S0.n)
    return nc, S, dumps


def host_inputs(inp):
    f32 = lambda a: np.ascontiguousarray(np.asarray(a, dtype=np.float32))
    gv = np.zeros((128, 176), np.float32)
    names = ["g_mix", "g_mix", "g_cross", "g_cross", "g_mem", "g_mem", "g_ffn", "g_ffn", "g_final"]
    for i, nm in enumerate(names):
        v = f32(inp[nm])
        v = v[i % 2] if v.ndim == 2 else v
        gv[:, i * 16:(i + 1) * 16] = v.reshape(16, 128).T
    for l in range(2):
        gv[:, 144 + l * 8:144 + (l + 1) * 8] = f32(inp["ret_gn"])[l].reshape(8, 128).T
        gv[:, 160 + l * 8:160 + (l + 1) * 8] = f32(inp["sinks"])[l][None, :]
    shared = {k: f32(inp[k]) for k in ("w_in", "w_out", "wq_c", "wk_c", "wv_c", "wo_c", "w_gate", "w_up", "w_down")}
    shared["gvec"] = gv
    shared["mem"] = f32(inp["mem_prompt"])[0]
    shared_flat = np.concatenate([shared[name].reshape(-1) for name, _ in IN_SPEC[N_PERCORE:]])
    maps = []
    xpf = f32(inp["x_prompt"])[0]
    xsf = f32(inp["x_sample"])[:, 0]
    for c in range(NCORES):
        m = {}
        bs = slice(c * NB, (c + 1) * NB)
        m["xp"] = xpf[c * NT:(c + 1) * NT]
        m["xs"] = xsf[bs]
        m["sret"] = f32(inp["state_ret"][:, bs])
        m["cwk"] = f32(inp["cache_win_k"][:, bs])
        m["cwv"] = f32(inp["cache_win_v"][:, bs])
        m["cmk"] = f32(inp["cache_mem_k"][:, bs])
        m["cmv"] = f32(inp["cache_mem_v"][:, bs])
        cFn, cBn = make_consts(c)
        m["cF"] = cFn
        m["cB"] = np.asarray(cBn, dtype=np.float32)
        flat = np.concatenate([m[name].reshape(-1) for name, _ in IN_SPEC[:N_PERCORE]] + [shared_flat])
        maps.append({"IN": flat})
    return maps


def unpack_out(flat):
    off, tot = _offsets(OUT_SPEC)
    flat = np.asarray(flat, dtype=np.float32).reshape(-1)
    return {name: flat[o:o + n].reshape(shape) for name, (o, n, shape) in off.items()}


_CACHE = {}


def kernel(**inputs):
    stage = 9
    if "prog" not in _CACHE:
        _CACHE["prog"] = make_program(stage, ())
    nc, S, _ = _CACHE["prog"]
    maps = host_inputs(inputs)
    used = set()
    for a in nc.allocations:
        if isinstance(a, mybir.MemoryLocationSet) and a.kind == "ExternalInput":
            used.add(a.memorylocations[0].name)
    maps = [{k: v for k, v in m.items() if k in used} for m in maps]
    res = run_bass_kernel_spmd(nc, maps, core_ids=list(range(NCORES)))
    R = [unpack_out(r["OUT"]) for r in res.results]
    f = lambda a: np.ascontiguousarray(np.asarray(a, dtype=np.float32))
    y_prompt = np.concatenate([f(R[c]["yp"]) for c in range(NCORES)], 0)[None]
    y_sample = np.concatenate([f(R[c]["ys"]) for c in range(NCORES)], 0)[:, None, :]
    ret_p = f(R[0]["retp"])[:, None]
    wk_p = f(R[7]["wkp"]).reshape(2, 1, 128, 2, 128)
    wv_p = f(R[7]["wvp"]).reshape(2, 1, 128, 2, 128)
    mk_p = f(R[0]["mkp"]).reshape(2, 1, 256, 4, 128)
    mv_p = f(R[0]["mvp"]).reshape(2, 1, 256, 4, 128)
    ret_s = np.concatenate([f(R[c]["rets"]) for c in range(NCORES)], 1)
    wk_s = np.concatenate([f(R[c]["wks"]) for c in range(NCORES)], 1).reshape(2, 128, 128, 2, 128)
    wv_s = np.concatenate([f(R[c]["wvs"]) for c in range(NCORES)], 1).reshape(2, 128, 128, 2, 128)
    return (y_prompt, y_sample, ret_p, wk_p, wv_p, mk_p, mv_p, ret_s, wk_s, wv_s)
```
